# Optimizing a Trainium2 kernel written in Bass

```python
import jax, jax.numpy as jnp
from jax import lax
import numpy as np

D_MODEL = 1024
BATCH = 8
SEQ = 2048
DEPTH = 2

MEM_LEN = 256
SB_HEADS = 8
SB_HEAD_DIM = 64
SB_WIDTH = SB_HEADS * SB_HEAD_DIM
SB_BLOCK = 128
ML_HEADS = 4
ML_HEAD_DIM = 128
ML_WIDTH = ML_HEADS * ML_HEAD_DIM
ML_CHUNK = 64
CONV_WIDTH = 4
MIX_WIDTH = SB_WIDTH + ML_WIDTH
N_IN = 3 * SB_WIDTH + 4 * ML_WIDTH + 2 * ML_HEADS
X_HEADS = 4
X_HEAD_DIM = D_MODEL // X_HEADS
D_FF = 2816
EPS = 1e-6

kernel_name = "hybrid_sb_mlstm_macaron_block"


def rms_norm(x, g):
    xf = x.astype(jnp.float32)
    y = xf * lax.rsqrt(jnp.mean(xf * xf, axis=-1, keepdims=True) + EPS)
    return (y * g.astype(jnp.float32)).astype(x.dtype)


def swiglu(x, w_gate, w_up, w_down):
    return (jax.nn.silu(x @ w_gate) * (x @ w_up)) @ w_down


def split_heads(a, n_heads, head_dim):
    b, s, _ = a.shape
    return a.reshape(b, s, n_heads, head_dim).transpose(0, 2, 1, 3)


def merge_heads(a):
    b, h, s, d = a.shape
    return a.transpose(0, 2, 1, 3).reshape(b, s, h * d)


def causal_depthwise_conv(u, w, b):
    k_width = w.shape[0]
    s = u.shape[1]
    up = jnp.pad(u, ((0, 0), (k_width - 1, 0), (0, 0)))
    y = b
    for j in range(k_width):
        y = y + up[:, j:j + s] * w[j]
    return y


def stick_breaking_attention(q, k, v):
    _, _, s_len, d = q.shape
    scale = d ** -0.5
    outs = []
    for blk in range(s_len // SB_BLOCK):
        q0 = blk * SB_BLOCK
        end = q0 + SB_BLOCK
        qb = q[:, :, q0:end].astype(jnp.float32)
        kb = k[:, :, :end].astype(jnp.float32)
        z = jnp.einsum('bhtd,bhsd->bhts', qb, kb) * scale
        t_idx = q0 + jnp.arange(SB_BLOCK)[:, None]
        s_idx = jnp.arange(end)[None, :]
        strict = s_idx < t_idx
        log_not = jnp.where(strict, jax.nn.log_sigmoid(-z), 0.0)
        later = lax.cumsum(log_not, axis=3, reverse=True) - log_not
        a = jnp.where(strict, jnp.exp(jax.nn.log_sigmoid(z) + later), 0.0)
        outs.append(jnp.einsum('bhts,bhsd->bhtd', a, v[:, :, :end].astype(jnp.float32)))
    return jnp.concatenate(outs, axis=2).astype(v.dtype)


def mlstm_chunkwise(q, k, v, i_pre, f_pre):
    b_sz, h_sz, s_len, d = q.shape
    n_chunks = s_len // ML_CHUNK
    qf = q.astype(jnp.float32) * (d ** -0.5)
    kf = k.astype(jnp.float32)
    vf = v.astype(jnp.float32)
    log_i = i_pre.astype(jnp.float32)
    log_f = jax.nn.log_sigmoid(f_pre.astype(jnp.float32))

    def to_chunks(a):
        return jnp.moveaxis(a.reshape(b_sz, h_sz, n_chunks, ML_CHUNK, *a.shape[3:]), 2, 0)

    causal = jnp.tril(jnp.ones((ML_CHUNK, ML_CHUNK), dtype=bool))

    def step(carry, xs):
        c_st, n_st, m_st = carry
        qc, kc, vc, li, lf = xs
        bcum = jnp.cumsum(lf, axis=-1)
        dmat = jnp.where(causal, bcum[..., :, None] - bcum[..., None, :] + li[..., None, :], -jnp.inf)
        inter = bcum + m_st[..., None]
        m_t = jnp.maximum(jnp.max(dmat, axis=-1), inter)
        w_intra = jnp.exp(dmat - m_t[..., None])
        w_inter = jnp.exp(inter - m_t)
        sc = jnp.einsum('bhtd,bhsd->bhts', qc, kc) * w_intra
        num = jnp.einsum('bhts,bhsd->bhtd', sc, vc) + w_inter[..., None] * jnp.einsum('bhtd,bhde->bhte', qc, c_st)
        den = jnp.sum(sc, axis=-1) + w_inter * jnp.einsum('bhtd,bhd->bht', qc, n_st)
        h = num / jnp.maximum(jnp.abs(den), jnp.exp(-m_t))[..., None]
        m_new = m_t[..., -1]
        w_state = jnp.exp(bcum[..., -1:] - bcum + li - m_new[..., None])
        decay = jnp.exp(bcum[..., -1] + m_st - m_new)
        c_new = decay[..., None, None] * c_st + jnp.einsum('bhs,bhsd,bhse->bhde', w_state, kc, vc)
        n_new = decay[..., None] * n_st + jnp.einsum('bhs,bhsd->bhd', w_state, kc)
        return (c_new, n_new, m_new), h

    init = (jnp.zeros((b_sz, h_sz, d, d), jnp.float32),
            jnp.zeros((b_sz, h_sz, d), jnp.float32),
            jnp.zeros((b_sz, h_sz), jnp.float32))
    xs = (to_chunks(qf), to_chunks(kf), to_chunks(vf), to_chunks(log_i), to_chunks(log_f))
    _, hs = lax.scan(step, init, xs)
    return jnp.moveaxis(hs, 0, 2).reshape(b_sz, h_sz, s_len, d).astype(q.dtype)


def memory_cross_attention(u, mem_n, w_xq, w_xk, w_xv, g_qnorm, g_knorm, w_xo):
    q = rms_norm(split_heads(u @ w_xq, X_HEADS, X_HEAD_DIM), g_qnorm)
    k = rms_norm(split_heads(mem_n @ w_xk, X_HEADS, X_HEAD_DIM), g_knorm)
    v = split_heads(mem_n @ w_xv, X_HEADS, X_HEAD_DIM)
    s = jnp.einsum('bhtd,bhmd->bhtm', q.astype(jnp.float32), k.astype(jnp.float32)) * (X_HEAD_DIM ** -0.5)
    p = jax.nn.softmax(s, axis=-1)
    o = jnp.einsum('bhtm,bhmd->bhtd', p, v.astype(jnp.float32)).astype(u.dtype)
    return merge_heads(o) @ w_xo


def hybrid_layer(x, mem, g_ff1, w_ff1_gate, w_ff1_up, w_ff1_down, g_mix, w_in, b_gate,
                 w_conv, b_conv, g_mlstm_head, w_out, g_xattn, g_mem, w_xq, w_xk, w_xv,
                 g_qnorm, g_knorm, w_xo, g_ff2, w_ff2_gate, w_ff2_up, w_ff2_down):
    x = x + 0.5 * swiglu(rms_norm(x, g_ff1), w_ff1_gate, w_ff1_up, w_ff1_down)

    u = rms_norm(x, g_mix)
    proj = u @ w_in
    sizes = (SB_WIDTH,) * 3 + (ML_WIDTH,) * 4 + (ML_HEADS, ML_HEADS)
    sb_q, sb_k, sb_v, ml_q, ml_k, ml_v, ml_o, ml_i, ml_f = jnp.split(
        proj, np.cumsum(sizes)[:-1].tolist(), axis=-1)

    sb = stick_breaking_attention(split_heads(sb_q, SB_HEADS, SB_HEAD_DIM),
                                  split_heads(sb_k, SB_HEADS, SB_HEAD_DIM),
                                  split_heads(sb_v, SB_HEADS, SB_HEAD_DIM))
    sb = merge_heads(sb)

    qk = jax.nn.silu(causal_depthwise_conv(jnp.concatenate([ml_q, ml_k], axis=-1), w_conv, b_conv))
    ml_qc, ml_kc = jnp.split(qk, 2, axis=-1)
    i_pre = (ml_i + b_gate[:ML_HEADS]).transpose(0, 2, 1)
    f_pre = (ml_f + b_gate[ML_HEADS:]).transpose(0, 2, 1)
    hm = mlstm_chunkwise(split_heads(ml_qc, ML_HEADS, ML_HEAD_DIM),
                         split_heads(ml_kc, ML_HEADS, ML_HEAD_DIM),
                         split_heads(ml_v, ML_HEADS, ML_HEAD_DIM), i_pre, f_pre)
    hm = rms_norm(hm, g_mlstm_head[:, None, :])
    ml = merge_heads(hm) * jax.nn.sigmoid(ml_o)

    x = x + jnp.concatenate([sb, ml], axis=-1) @ w_out

    x = x + memory_cross_attention(rms_norm(x, g_xattn), rms_norm(mem, g_mem),
                                   w_xq, w_xk, w_xv, g_qnorm, g_knorm, w_xo)

    x = x + 0.5 * swiglu(rms_norm(x, g_ff2), w_ff2_gate, w_ff2_up, w_ff2_down)
    return x


def setup_inputs(seed: int = 0) -> dict:
    key = jax.random.key(seed)
    ks = jax.random.split(key, 32)
    f32 = jnp.float32

    def w(k, shape, fan_in):
        return jax.random.normal(k, shape, f32) * (fan_in ** -0.5)

    def gain(k, shape):
        return 1.0 + 0.02 * jax.random.normal(k, shape, f32)

    L = DEPTH
    i_bias = 0.1 * jax.random.normal(ks[8], (L, ML_HEADS), f32)
    f_bias = 3.0 + 0.5 * jax.random.normal(ks[9], (L, ML_HEADS), f32)
    return {
        "x": jax.random.normal(ks[0], (BATCH, SEQ, D_MODEL), f32),
        "mem": jax.random.normal(ks[1], (BATCH, MEM_LEN, D_MODEL), f32),
        "g_ff1": gain(ks[2], (L, D_MODEL)),
        "w_ff1_gate": w(ks[3], (L, D_MODEL, D_FF), D_MODEL),
        "w_ff1_up": w(ks[4], (L, D_MODEL, D_FF), D_MODEL),
        "w_ff1_down": w(ks[5], (L, D_FF, D_MODEL), D_FF),
        "g_mix": gain(ks[6], (L, D_MODEL)),
        "w_in": w(ks[7], (L, D_MODEL, N_IN), D_MODEL),
        "b_gate": jnp.concatenate([i_bias, f_bias], axis=-1),
        "w_conv": w(ks[10], (L, CONV_WIDTH, 2 * ML_WIDTH), CONV_WIDTH),
        "b_conv": 0.02 * jax.random.normal(ks[11], (L, 2 * ML_WIDTH), f32),
        "g_mlstm_head": gain(ks[12], (L, ML_HEADS, ML_HEAD_DIM)),
        "w_out": w(ks[13], (L, MIX_WIDTH, D_MODEL), MIX_WIDTH),
        "g_xattn": gain(ks[14], (L, D_MODEL)),
        "g_mem": gain(ks[15], (L, D_MODEL)),
        "w_xq": w(ks[16], (L, D_MODEL, D_MODEL), D_MODEL),
        "w_xk": w(ks[17], (L, D_MODEL, D_MODEL), D_MODEL),
        "w_xv": w(ks[18], (L, D_MODEL, D_MODEL), D_MODEL),
        "g_qnorm": gain(ks[19], (L, X_HEAD_DIM)),
        "g_knorm": gain(ks[20], (L, X_HEAD_DIM)),
        "w_xo": w(ks[21], (L, D_MODEL, D_MODEL), D_MODEL),
        "g_ff2": gain(ks[22], (L, D_MODEL)),
        "w_ff2_gate": w(ks[23], (L, D_MODEL, D_FF), D_MODEL),
        "w_ff2_up": w(ks[24], (L, D_MODEL, D_FF), D_MODEL),
        "w_ff2_down": w(ks[25], (L, D_FF, D_MODEL), D_FF),
    }


def reference(x, mem, g_ff1, w_ff1_gate, w_ff1_up, w_ff1_down, g_mix, w_in, b_gate,
              w_conv, b_conv, g_mlstm_head, w_out, g_xattn, g_mem, w_xq, w_xk, w_xv,
              g_qnorm, g_knorm, w_xo, g_ff2, w_ff2_gate, w_ff2_up, w_ff2_down):
    for l in range(DEPTH):
        x = hybrid_layer(x, mem, g_ff1[l], w_ff1_gate[l], w_ff1_up[l], w_ff1_down[l],
                         g_mix[l], w_in[l], b_gate[l], w_conv[l], b_conv[l],
                         g_mlstm_head[l], w_out[l], g_xattn[l], g_mem[l], w_xq[l],
                         w_xk[l], w_xv[l], g_qnorm[l], g_knorm[l], w_xo[l], g_ff2[l],
                         w_ff2_gate[l], w_ff2_up[l], w_ff2_down[l])
    return x
```

```python
import contextlib
import types
import numpy as np
import concourse.bass as bass
import concourse.mybir as mybir
from concourse.bass_utils import run_bass_kernel_spmd

F32 = mybir.dt.float32
BF16 = mybir.dt.bfloat16
ALU = mybir.AluOpType
AF = mybir.ActivationFunctionType

S = 2048
D = 1024
NT = 16
NTB = 4
KC = 8
DFF = 2816
NIN = 3592
MEM = 256
EPS = 1e-6
DEPTH = 2
NCORES = 8
NCV = 64

QUEUES = ("pe", "act", "dve", "pool", "sp")


class Buf:
    __slots__ = ("name", "lw", "lr", "sem", "semcount")

    def __init__(self, name=""):
        self.name = name
        self.lw = {}
        self.lr = {}
        self.sem = None
        self.semcount = 0


class Op:
    __slots__ = ("q", "fn", "deps", "signal", "sem", "val", "dma", "key", "idx")


def _freeze(fn):
    if fn.__closure__ is None:
        return fn
    cells = []
    for c in fn.__closure__:
        try:
            cells.append(types.CellType(c.cell_contents))
        except ValueError:
            cells.append(c)
    return types.FunctionType(fn.__code__, fn.__globals__, fn.__name__, fn.__defaults__, tuple(cells))


class Prog:
    def __init__(self, nc, stack):
        self.nc = nc
        self.stack = stack
        self.ops = []
        self.byq = {q: [] for q in QUEUES}
        self.nsem = 0
        self._dmakey = 0
        self.last = {}
        self.pending_barrier = {}

    def new_sem(self, name):
        self.nsem += 1
        return self.stack.enter_context(self.nc.semaphore("s%d_%s" % (self.nsem, name)))

    def barrier(self):
        lasts = set(self.last.values())
        for q in QUEUES:
            self.pending_barrier[q] = set(lasts)

    def add(self, q, fn, reads=(), writes=(), dma=False, sem_buf=None):
        op = Op()
        op.q = q
        op.fn = _freeze(fn)
        op.dma = dma
        op.signal = dma
        op.sem = None
        op.val = 0
        op.idx = len(self.ops)
        if dma:
            self._dmakey += 1
            op.key = ("dma", self._dmakey)
        else:
            op.key = q
        deps = set()
        for b in reads:
            for k, w in b.lw.items():
                deps.add(w)
        for b in writes:
            for k, r in b.lr.items():
                if dma or r.dma or k != op.key:
                    deps.add(r)
            for k, w in b.lw.items():
                if dma or w.dma or k != op.key:
                    deps.add(w)
        pb = self.pending_barrier.pop(q, None)
        if pb:
            deps |= pb
        op.deps = deps
        for b in reads:
            b.lr[op.key] = op
        for b in writes:
            b.lw = {op.key: op}
            b.lr = {}
        if dma:
            sb = sem_buf if sem_buf is not None else writes[0]
            if sb.sem is None:
                sb.sem = self.new_sem("d")
            sb.semcount += 16
            op.sem = sb.sem
            op.val = sb.semcount
        self.ops.append(op)
        self.byq[q].append(op)
        self.last[op.key if not dma else ("dmaq", q, id(op.sem))] = op
        return op

    def pe(self, fn, reads=(), writes=()):
        return self.add("pe", fn, reads, writes)

    def act(self, fn, reads=(), writes=()):
        return self.add("act", fn, reads, writes)

    def dve(self, fn, reads=(), writes=()):
        return self.add("dve", fn, reads, writes)

    def pool(self, fn, reads=(), writes=()):
        return self.add("pool", fn, reads, writes)

    def dma(self, q, fn, reads=(), writes=(), sem_buf=None):
        return self.add(q, fn, reads, writes, dma=True, sem_buf=sem_buf)

    def finalize(self, final_wait_bufs=()):
        nc = self.nc
        for op in self.ops:
            for d in op.deps:
                d.signal = True
        LIM = 30000
        for q in QUEUES:
            sem = None
            cnt = 0
            for op in self.byq[q]:
                if op.dma or not op.signal:
                    continue
                if sem is None or cnt >= LIM:
                    sem = self.new_sem("c_" + q)
                    cnt = 0
                cnt += 1
                op.sem = sem
                op.val = cnt
        finals = [(b.sem, b.semcount) for b in final_wait_bufs]

        def run_queue(q, eng):
            waited = {}
            for op in self.byq[q]:
                need = {}
                for d in op.deps:
                    s = d.sem
                    cur = need.get(id(s))
                    if cur is None or cur[1] < d.val:
                        need[id(s)] = (s, d.val)
                for sid, (s, v) in need.items():
                    if waited.get(sid, 0) < v:
                        eng.wait_ge(s, v)
                        waited[sid] = v
                ins = op.fn(eng)
                if op.signal:
                    ins.then_inc(op.sem, 16 if op.dma else 1)
            if q == "sp":
                for s, v in finals:
                    eng.wait_ge(s, v)

        with nc.Block() as block:
            @block.tensor
            def _(e):
                run_queue("pe", e)

            @block.scalar
            def _(e):
                run_queue("act", e)

            @block.vector
            def _(e):
                run_queue("dve", e)

            @block.gpsimd
            def _(e):
                run_queue("pool", e)

            @block.sync
            def _(e):
                run_queue("sp", e)


class Arena:
    def __init__(self, tensor, nbytes):
        self.t = tensor
        self.n = nbytes
        self.off = 0

    def reset(self, off=0):
        self.off = off

    def alloc(self, shape, dtype):
        esz = 4 if dtype == F32 else 2
        n = 1
        for s in shape:
            n *= s
        nb = (n * esz + 31) // 32 * 32
        assert self.off + nb <= self.n, ("arena overflow", self.off, nb, self.n)
        v = self.t[:, self.off // 2:(self.off + n * esz) // 2]
        self.off += nb
        if dtype == F32:
            v = v.bitcast(F32)
        if len(shape) == 2:
            v = v.rearrange("p (a b) -> p a b", a=shape[0])
        elif len(shape) == 3:
            v = v.rearrange("p (a b c) -> p a b c", a=shape[0], b=shape[1])
        return v


class Rot:
    def __init__(self, items):
        self.items = items
        self.i = 0

    def next(self):
        it = self.items[self.i % len(self.items)]
        self.i += 1
        return it


PARAMS = [
    ("g_ff1", [D]), ("w_ff1_gate", [D, DFF]), ("w_ff1_up", [D, DFF]), ("w_ff1_down", [DFF, D]),
    ("g_mix", [D]), ("w_in", [D, NIN]), ("wgp", [D, 256]), ("cvec", [128, NCV]),
    ("w_out", [D, D]), ("g_xattn", [D]), ("g_mem", [D]), ("w_xq", [D, D]), ("w_xk", [D, D]),
    ("w_xv", [D, D]), ("w_xo", [D, D]), ("g_ff2", [D]), ("w_ff2_gate", [D, DFF]),
    ("w_ff2_up", [D, DFF]), ("w_ff2_down", [DFF, D]),
]
CV_BCONV = 0
CV_WCONV = 8
CV_GHEAD = 40
CV_GQ = 44
CV_GK = 46
CV_BI = 48
CV_BF = 49


def build_program(L, stop_after=None):
    nc = bass.Bass("TRN2", target_bir_lowering=False)

    def din(name, shape):
        return nc.dram_tensor(name, shape, F32, kind="ExternalInput").ap()

    x_d = din("x", [S, D])
    mem_d = din("mem", [MEM, D])
    Wd = {name: din(name, [L] + shape) for name, shape in PARAMS}
    out_d = nc.dram_tensor("out", [S, D], F32, kind="ExternalOutput").ap()

    with contextlib.ExitStack() as st:
        P = Prog(nc, st)

        def sb(name, shape, dt):
            return st.enter_context(nc.sbuf_tensor(name, shape, dt))

        x_sb = sb("x_sb", [128, NT, D], F32)
        xnT = sb("xnT", [128, KC, S], BF16)
        ARENA_BYTES = 104 * 1024
        arena_t = sb("arena", [128, ARENA_BYTES // 2], BF16)
        AR = Arena(arena_t, ARENA_BYTES)
        ident = sb("ident", [128, 128], BF16)
        m_strict = sb("m_strict", [128, 128], BF16)
        m_incl = sb("m_incl", [128, 128], BF16)
        l_incl = sb("l_incl", [128, 128], BF16)
        ones_bf = sb("ones_bf", [128, 128], BF16)
        zeros_bf = sb("zeros_bf", [128, 128], BF16)
        ones_f = sb("ones_f", [128, 512], F32)
        zeros_f = sb("zeros_f", [128, 1], F32)
        eps_t = sb("eps_t", [128, 1], F32)
        selc = sb("selc", [128, 4], F32)
        selh = [sb("selh%d" % h, [128, 128], F32) for h in range(4)]
        cscr = sb("cscr", [128, 128], F32)
        Bconst = Buf("const")
        Bcscr = Buf("cscr")

        pbank = [st.enter_context(nc.psum_tensor("pb%d" % i, [128, 512], F32)) for i in range(8)]
        Bpb = [Buf("pb%d" % i) for i in range(8)]

        Bx = [[Buf("x%d_%d" % (t, h)) for h in range(2)] for t in range(NT)]
        BxT = [Buf("xnT%d" % t) for t in range(NT)]
        Bout = Buf("out")

        def mk_mask(dst, cm, pat, op):
            P.pool(lambda e: e.memset(cscr[:], 1.0), writes=[Bcscr])
            P.pool(lambda e: e.affine_select(out=cscr[:], in_=cscr[:], pattern=[[pat, 128]], compare_op=op,
                                             fill=0.0, base=0, channel_multiplier=cm), reads=[Bcscr], writes=[Bcscr])
            P.dve(lambda e: e.tensor_copy(out=dst[:], in_=cscr[:]), reads=[Bcscr], writes=[Bconst])

        mk_mask(ident, 1, -1, ALU.is_equal)
        mk_mask(m_strict, -1, 1, ALU.is_gt)
        mk_mask(m_incl, -1, 1, ALU.is_ge)
        mk_mask(l_incl, 1, -1, ALU.is_ge)
        P.dve(lambda e: e.memset(ones_bf[:], 1.0), writes=[Bconst])
        P.dve(lambda e: e.memset(zeros_bf[:], 0.0), writes=[Bconst])
        P.dve(lambda e: e.memset(ones_f[:], 1.0), writes=[Bconst])
        P.dve(lambda e: e.memset(zeros_f[:], 0.0), writes=[Bconst])
        P.dve(lambda e: e.memset(eps_t[:], EPS), writes=[Bconst])
        P.pool(lambda e: e.memset(selc[:], 1.0), writes=[Bconst])
        P.pool(lambda e: e.affine_select(out=selc[:], in_=selc[:], pattern=[[-32, 4]], compare_op=ALU.is_equal,
                                         fill=0.0, base=0, channel_multiplier=1), reads=[Bconst], writes=[Bconst])
        for h in range(4):
            P.pool(lambda e, h=h: e.memset(selh[h][:], 1.0), writes=[Bconst])
            P.pool(lambda e, h=h: e.affine_select(out=selh[h][:], in_=selh[h][:], pattern=[[0, 128]], compare_op=ALU.is_equal,
                                                  fill=0.0, base=-32 * h, channel_multiplier=1), reads=[Bconst], writes=[Bconst])

        xv = x_d.rearrange("(t p) d -> p t d", p=128)
        ov = out_d.rearrange("(t p) d -> p t d", p=128)
        for c in range(4):
            bl = Buf("xload%d" % c)
            P.dma("sp", lambda e, c=c: e.dma_start(out=x_sb[:, 4 * c:4 * c + 4, :], in_=xv[:, 4 * c:4 * c + 4, :]),
                  writes=[Bx[t][h] for t in range(4 * c, 4 * c + 4) for h in range(2)], sem_buf=bl)

        def wdma(dst, src, buf):
            P.dma("pool", lambda e: e.dma_start(out=dst, in_=src), writes=[buf])

        def sdma(dst, src, buf):
            P.dma("sp", lambda e: e.dma_start(out=dst, in_=src), writes=[buf])

        def rstd_from_ms(dst, src, rd, wr, scale=1.0):
            P.act(lambda e: e.activation(out=dst, in_=src, func=AF.Ln, bias=eps_t[:, 0:1], scale=scale), reads=rd, writes=wr)
            P.act(lambda e: e.activation(out=dst, in_=dst, func=AF.Exp, scale=-0.5), reads=wr, writes=wr)


        def norm_T(g_row, src_tile, src_bufs, ntile, dstT, dst_bufs, tok0=0):
            mark = AR.off
            gb = AR.alloc([D], F32)
            Bgb = Buf("gb")
            ss = AR.alloc([NT], F32)
            rs = AR.alloc([NT], F32)
            Bss = Buf("ss")
            junk = AR.alloc([D], BF16)
            Bjunk = Buf("junk")
            xns = Rot([(AR.alloc([D], BF16), Buf("xn%d" % i)) for i in range(2)])
            sdma(gb, g_row.partition_broadcast(128), Bgb)
            P.dve(lambda e: e.memset(ss, 0.0), writes=[Bss])
            for tt in range(ntile):
                P.act(lambda e, tt=tt: e.activation(out=junk, in_=src_tile(tt), func=AF.Square, scale=1.0 / 32.0,
                                                    accum_out=ss[:, tt:tt + 1]),
                      reads=src_bufs(tt) + [Bss], writes=[Bjunk, Bss])
            rstd_from_ms(rs[:, 0:ntile], ss[:, 0:ntile], [Bss], [Bss])
            pts = Rot([(pbank[6], Bpb[6]), (pbank[7], Bpb[7])])
            for tt in range(ntile):
                xn, Bxn = xns.next()
                P.dve(lambda e, tt=tt, xn=xn: e.scalar_tensor_tensor(out=xn, in0=src_tile(tt), scalar=rs[:, tt:tt + 1],
                                                                     in1=gb, op0=ALU.mult, op1=ALU.mult),
                      reads=src_bufs(tt) + [Bss, Bgb], writes=[Bxn])
                pt, Bpt = pts.next()
                ptb = pt[:, :].bitcast(BF16)
                for c in range(KC):
                    P.pe(lambda e, c=c, xn=xn, ptb=ptb: e.transpose(ptb[:, c * 128:(c + 1) * 128], xn[:, c * 128:(c + 1) * 128], ident[:]),
                         reads=[Bxn, Bconst], writes=[Bpt])
                P.dve(lambda e, tt=tt, ptb=ptb: e.tensor_copy(out=dstT[:, :, tok0 + tt * 128: tok0 + (tt + 1) * 128],
                                                              in_=ptb.rearrange("p (c t) -> p c t", c=KC)),
                      reads=[Bpt], writes=[dst_bufs[tt]])
            return mark

        def x_tile(tt):
            return x_sb[:, tt, :]

        def x_bufs(tt):
            return [Bx[tt][0], Bx[tt][1]]

        def xT_bufs(tb):
            return [BxT[4 * tb + i] for i in range(4)]

        def ffn(l, g_name, wg_name, wu_name, wd_name):
            P.barrier()
            AR.reset()
            norm_T(Wd[g_name][l:l + 1, :], x_tile, x_bufs, NT, xnT, BxT)
            P.barrier()
            AR.reset()
            BLK = 256
            NB = DFF // BLK
            groups = [[0, 1, 2, 3], [4, 5, 6, 7], [8, 9, 10]]
            GMAX = 8
            hT = AR.alloc([GMAX, S], BF16)
            BhT = [[Buf("hT%d_%d" % (j, tb)) for tb in range(NTB)] for j in range(GMAX)]
            wdg = AR.alloc([GMAX, D], BF16)
            Bwd = [Buf("wd%d" % j) for j in range(GMAX // 2)]
            NSL = 3
            wgs = [(AR.alloc([KC, BLK], BF16), Buf("wg%d" % i)) for i in range(NSL)]
            wus = [(AR.alloc([KC, BLK], BF16), Buf("wu%d" % i)) for i in range(NSL)]
            sgs = Rot([(AR.alloc([512], F32), Buf("sg%d" % i)) for i in range(2)])
            wgv = Wd[wg_name][l].rearrange("(c p) n -> p c n", p=128)
            wuv = Wd[wu_name][l].rearrange("(c p) n -> p c n", p=128)
            wdv = Wd[wd_name][l].rearrange("(c p) d -> p c d", p=128)

            def load_blk(b):
                s = b % NSL
                wdma(wgs[s][0], wgv[:, :, b * BLK:(b + 1) * BLK], wgs[s][1])
                wdma(wus[s][0], wuv[:, :, b * BLK:(b + 1) * BLK], wus[s][1])

            PF = 2
            for b in range(PF):
                load_blk(b)
            gps = Rot([(pbank[0], Bpb[0]), (pbank[1], Bpb[1])])
            ups = Rot([(pbank[2], Bpb[2]), (pbank[3], Bpb[3])])
            yps = Rot([(pbank[4], Bpb[4]), (pbank[5], Bpb[5])])
            for grp in groups:
                for gi, b in enumerate(grp):
                    wdma(wdg[:, 2 * gi:2 * gi + 2, :], wdv[:, 2 * b:2 * b + 2, :], Bwd[gi])
                for gi, b in enumerate(grp):
                    if b + PF < NB:
                        load_blk(b + PF)
                    s = b % NSL
                    wg, Bwg = wgs[s]
                    wu, Bwu = wus[s]
                    for jj in range(2):
                        j = 2 * gi + jj
                        for tb in range(NTB):
                            gp, Bgp = gps.next()
                            up, Bup = ups.next()
                            for k in range(KC):
                                P.pe(lambda e, k=k, gp=gp, wg=wg, jj=jj, tb=tb: e.matmul(
                                    gp[:, :], lhsT=wg[:, k, jj * 128:(jj + 1) * 128], rhs=xnT[:, k, tb * 512:(tb + 1) * 512],
                                    start=(k == 0), stop=(k == KC - 1)), reads=[Bwg] + xT_bufs(tb), writes=[Bgp])
                            for k in range(KC):
                                P.pe(lambda e, k=k, up=up, wu=wu, jj=jj, tb=tb: e.matmul(
                                    up[:, :], lhsT=wu[:, k, jj * 128:(jj + 1) * 128], rhs=xnT[:, k, tb * 512:(tb + 1) * 512],
                                    start=(k == 0), stop=(k == KC - 1)), reads=[Bwu] + xT_bufs(tb), writes=[Bup])
                            sg, Bsg = sgs.next()
                            P.act(lambda e, sg=sg, gp=gp: e.activation(out=sg, in_=gp[:, :], func=AF.Silu),
                                  reads=[Bgp], writes=[Bsg])
                            P.dve(lambda e, sg=sg, up=up, j=j, tb=tb: e.tensor_tensor(
                                out=hT[:, j, tb * 512:(tb + 1) * 512], in0=sg, in1=up[:, :], op=ALU.mult),
                                reads=[Bsg, Bup], writes=[BhT[j][tb]])
                nj = 2 * len(grp)
                for tt in range(NT):
                    for dh in range(2):
                        yp, Byp = yps.next()
                        for j in range(nj):
                            P.pe(lambda e, j=j, yp=yp, tt=tt, dh=dh: e.matmul(
                                yp[:, :], lhsT=hT[:, j, tt * 128:(tt + 1) * 128], rhs=wdg[:, j, dh * 512:(dh + 1) * 512],
                                start=(j == 0), stop=(j == nj - 1)),
                                reads=[BhT[j][tt // 4], Bwd[j // 2]], writes=[Byp])
                        P.dve(lambda e, yp=yp, tt=tt, dh=dh: e.scalar_tensor_tensor(
                            out=x_sb[:, tt, dh * 512:(dh + 1) * 512], in0=yp[:, :], scalar=0.5,
                            in1=x_sb[:, tt, dh * 512:(dh + 1) * 512], op0=ALU.mult, op1=ALU.add),
                            reads=[Byp, Bx[tt][dh]], writes=[Bx[tt][dh]])

        def mixer(l):
            P.barrier()
            AR.reset()
            norm_T(Wd["g_mix"][l:l + 1, :], x_tile, x_bufs, NT, xnT, BxT)
            P.barrier()
            AR.reset()
            winv = Wd["w_in"][l].rearrange("(c p) n -> p c n", p=128)
            woutv = Wd["w_out"][l].rearrange("(c p) d -> p c d", p=128)
            cv = AR.alloc([NCV], F32)
            Bcv = Buf("cv")
            sdma(cv, Wd["cvec"][l], Bcv)
            vsl = Rot([(AR.alloc([NT, 128], BF16), Buf("v%d" % i)) for i in range(2)])
            mixs = Rot([(AR.alloc([S], BF16), [Buf("mix%d_%d" % (i, tb)) for tb in range(NTB)]) for i in range(2)])
            wsl = Rot([(AR.alloc([KC, 128], BF16), Buf("wsl%d" % i)) for i in range(6)])
            wos = Rot([(AR.alloc([D], BF16), Buf("wo%d" % i)) for i in range(2)])
            common_end = AR.off
            prj = Rot([(pbank[6], Bpb[6]), (pbank[7], Bpb[7])])

            class BG:
                def __init__(self):
                    self.q = []
                    self.timed = []

                def add(self, th):
                    self.q.append(th)

                def add_timed(self, delay, th):
                    self.timed.append([delay, th])

                def step(self, n=1):
                    fire = [t for t in self.timed if t[0] <= 0]
                    self.timed = [t for t in self.timed if t[0] > 0]
                    for t in self.timed:
                        t[0] -= 1
                    for t in fire:
                        t[1]()
                    for _ in range(n):
                        if self.q:
                            self.q.pop(0)()

                def step_auto(self, remaining):
                    n = -(-len(self.q) // max(1, remaining))
                    self.step(max(1, n) if self.q else 0)

                def flush(self):
                    for t in self.timed:
                        t[1]()
                    self.timed = []
                    while self.q:
                        self.q.pop(0)()

            bg = BG()

            def proj_fm(col0, evac):
                w, Bw = wsl.next()
                first = [True]

                def mk(tb):
                    hold = {}

                    def sub(j):
                        def th():
                            if first[0]:
                                wdma(w, winv[:, :, col0:col0 + 128], Bw)
                                first[0] = False
                            if j == 0:
                                hold["pp"] = prj.next()
                            pp, Bpp = hold["pp"]
                            for k in (2 * j, 2 * j + 1):
                                P.pe(lambda e, k=k: e.matmul(pp[:, :], lhsT=w[:, k, :], rhs=xnT[:, k, tb * 512:(tb + 1) * 512],
                                                             start=(k == 0), stop=(k == KC - 1)), reads=[Bw] + xT_bufs(tb), writes=[Bpp])
                            if j == 3:
                                evac(tb, pp, Bpp)
                        return th
                    return [sub(j) for j in range(4)]
                for tb in range(NTB):
                    for th in mk(tb):
                        bg.add(th)

            def proj_tm(col0, v, Bv):
                w, Bw = wsl.next()
                first = [True]

                def mk(t4):
                    hold = {}

                    def sub(i):
                        def th():
                            if first[0]:
                                wdma(w, winv[:, :, col0:col0 + 128], Bw)
                                first[0] = False
                            if i == 0:
                                hold["pp"] = prj.next()
                            pp, Bpp = hold["pp"]
                            tt = t4 * 4 + i
                            for k in range(KC):
                                P.pe(lambda e, k=k: e.matmul(
                                    pp[:, i * 128:(i + 1) * 128], lhsT=xnT[:, k, tt * 128:(tt + 1) * 128], rhs=w[:, k, :],
                                    start=(k == 0), stop=(k == KC - 1)), reads=[Bw, BxT[tt]], writes=[Bpp])
                            if i == 3:
                                P.act(lambda e: e.copy(out=v[:, t4 * 4:t4 * 4 + 4, :], in_=pp[:, :].rearrange("p (a b) -> p a b", a=4)),
                                      reads=[Bpp], writes=[Bv])
                        return th
                    return [sub(i) for i in range(4)]
                for t4 in range(NT // 4):
                    for th in mk(t4):
                        bg.add(th)

            def out_proj(c, mix, Bmix):
                wo, Bwo = wos.next()
                first = [True]

                def mk(tt, dh):
                    def th():
                        if first[0]:
                            wdma(wo, woutv[:, c, :], Bwo)
                            first[0] = False
                        yp, Byp = prj.next()
                        P.pe(lambda e: e.matmul(yp[:, :], lhsT=mix[:, tt * 128:(tt + 1) * 128], rhs=wo[:, dh * 512:(dh + 1) * 512],
                                                start=True, stop=True), reads=[Bmix[tt // 4], Bwo], writes=[Byp])
                        P.dve(lambda e: e.tensor_tensor(out=x_sb[:, tt, dh * 512:(dh + 1) * 512], in0=yp[:, :],
                                                        in1=x_sb[:, tt, dh * 512:(dh + 1) * 512], op=ALU.add),
                              reads=[Byp, Bx[tt][dh]], writes=[Bx[tt][dh]])
                    return th
                for tt in range(NT):
                    for dh in range(2):
                        bg.add(mk(tt, dh))

            qks = Rot([((AR.alloc([S], BF16), [Buf("q%d_%d" % (i, tb)) for tb in range(NTB)]),
                        (AR.alloc([S], BF16), [Buf("k%d_%d" % (i, tb)) for tb in range(NTB)])) for i in range(2)])
            Es = Rot([(AR.alloc([512], F32), Buf("E%d" % i)) for i in range(4)])
            Sps = Rot([(AR.alloc([512], BF16), Buf("Sp%d" % i)) for i in range(4)])
            Xs = Rot([(AR.alloc([512], F32), Buf("X%d" % i)) for i in range(2)])
            As = Rot([(AR.alloc([512], BF16), Buf("A%d" % i)) for i in range(3)])
            zps = Rot([(pbank[0], Bpb[0]), (pbank[1], Bpb[1]), (pbank[5], Bpb[5])])
            cps = [(pbank[2], Bpb[2]), (pbank[3], Bpb[3])]
            ops_ = (pbank[4], Bpb[4])
            zsrc_bufs = [BxT[0], BxT[1], BxT[2], BxT[3]]

            def evac_to(dst, Bdst):
                def f(tb, pp, Bpp):
                    P.act(lambda e: e.copy(out=dst[:, tb * 512:(tb + 1) * 512], in_=pp[:, :]), reads=[Bpp], writes=[Bdst[tb]])
                return f

            def sb_sched_proj(p):
                (qT, BqT), (kT, BkT) = qks.next()
                v, Bv = vsl.next()
                proj_fm(0 + p * 128, evac_to(qT, BqT))
                proj_fm(512 + p * 128, evac_to(kT, BkT))
                proj_tm(1024 + p * 128, v, Bv)
                return qT, BqT, kT, BkT, v, Bv

            def sb_attention(qT, BqT, kT, BkT, v, Bv, mix, Bmix):
                for tb in range(NTB):
                    op_, Bop = ops_
                    P.pe(lambda e: e.matmul(op_[:, :], lhsT=zeros_bf[:, :], rhs=xnT[:, 0, 0:512], start=True, stop=False),
                         reads=[Bconst] + zsrc_bufs, writes=[Bop])
                    for e_ in range(2):
                        cp, Bcp = cps[e_]
                        P.pe(lambda e, cp=cp: e.matmul(cp[:, :], lhsT=zeros_bf[:, :], rhs=xnT[:, 0, 0:512], start=True, stop=True),
                             reads=[Bconst] + zsrc_bufs, writes=[Bcp])
                    units = []
                    for c in range(4 * tb + 3, -1, -1):
                        for e_ in range(2):
                            units.append((c, e_))
                    nU = len(units)
                    state = [None] * nU

                    def stage0(u):
                        c, e_ = units[u]
                        r = c - 4 * tb
                        off = 128 * r if r >= 0 else 0
                        zp, Bzp = zps.next()
                        E, BE = Es.next()
                        Sp, BSp = Sps.next()
                        hs = slice(64 * e_, 64 * e_ + 64)
                        P.pe(lambda e: e.matmul(zp[:, off:512], lhsT=kT[hs, c * 128:(c + 1) * 128],
                                                rhs=qT[hs, tb * 512 + off:(tb + 1) * 512], start=True, stop=True),
                             reads=[BkT[c // 4], BqT[tb]], writes=[Bzp])
                        P.act(lambda e: e.activation(out=E[:, off:512], in_=zp[:, off:512], func=AF.Exp, scale=0.125),
                              reads=[Bzp], writes=[BE])
                        P.act(lambda e: e.activation(out=Sp[:, off:512], in_=E[:, off:512], func=AF.Ln, bias=1.0),
                              reads=[BE], writes=[BSp])
                        if r >= 0:
                            P.dve(lambda e: e.tensor_tensor(out=Sp[:, off:off + 128], in0=Sp[:, off:off + 128],
                                                            in1=m_strict[:, :], op=ALU.mult),
                                  reads=[BSp, Bconst], writes=[BSp])
                        state[u] = (c, e_, r, off, E, BE, Sp, BSp)

                    def stage1(u):
                        c, e_, r, off, E, BE, Sp, BSp = state[u]
                        cp, Bcp = cps[e_]
                        X, BX = Xs.next()
                        A, BA = As.next()
                        P.pe(lambda e: e.matmul(cp[:, off:512], lhsT=l_incl[:, :], rhs=Sp[:, off:512], start=False, stop=True),
                             reads=[Bconst, BSp], writes=[Bcp])
                        P.act(lambda e: e.activation(out=X[:, off:512], in_=cp[:, off:512], func=AF.Exp, scale=-1.0),
                              reads=[Bcp], writes=[BX])
                        P.dve(lambda e: e.tensor_tensor(out=A[:, off:512], in0=E[:, off:512], in1=X[:, off:512], op=ALU.mult),
                              reads=[BE, BX], writes=[BA])
                        if r >= 0:
                            P.dve(lambda e: e.tensor_tensor(out=A[:, off:off + 128], in0=A[:, off:off + 128],
                                                            in1=m_strict[:, :], op=ALU.mult),
                                  reads=[BA, Bconst], writes=[BA])
                        state[u] = state[u] + (A, BA)

                    def stage2(u):
                        c, e_, r, off, E, BE, Sp, BSp, A, BA = state[u]
                        cp, Bcp = cps[e_]
                        if c > 0:
                            P.pe(lambda e: e.matmul(cp[:, off:512], lhsT=m_strict[:, :], rhs=Sp[:, off:512], start=False, stop=True),
                                 reads=[Bconst, BSp], writes=[Bcp])
                        P.pe(lambda e: e.matmul(op_[64 * e_:64 * e_ + 64, off:512], lhsT=v[:, c, 64 * e_:64 * e_ + 64],
                                                rhs=A[:, off:512], start=False, stop=(c == 0)),
                             reads=[Bv, BA], writes=[Bop])

                    for step in range(nU + 3):
                        if step < nU:
                            stage0(step)
                        if 0 <= step - 2 < nU:
                            stage1(step - 2)
                        if 0 <= step - 3 < nU:
                            stage2(step - 3)
                        sb_left[0] -= 1
                        bg.step_auto(sb_left[0])
                    P.act(lambda e: e.copy(out=mix[:, tb * 512:(tb + 1) * 512], in_=op_[:, :]),
                          reads=[Bop], writes=[Bmix[tb]])

            sb_left = [92]
            if stop_after != "skip_sb":
                nxt = sb_sched_proj(0)
                bg.flush()
                for p in range(4):
                    cur = nxt
                    if p + 1 < 4:
                        nxt = sb_sched_proj(p + 1)
                    mix, Bmix = mixs.next()
                    sb_left[0] = 92
                    sb_attention(*cur, mix, Bmix)
                    bg.flush()
                    out_proj(p, mix, Bmix)
                bg.flush()

            P.barrier()
            AR.reset(common_end)
            T1 = AR.alloc([S], F32)
            T2 = AR.alloc([S], F32)
            T3 = AR.alloc([S], F32)
            BT1 = [Buf("T1_%d" % tb) for tb in range(NTB)]
            BT2 = [Buf("T2_%d" % tb) for tb in range(NTB)]
            BT3 = [Buf("T3_%d" % tb) for tb in range(NTB)]
            wgp = AR.alloc([KC, 256], BF16)
            Bwgp = Buf("wgp")
            qa = AR.alloc([S], BF16)
            ka = AR.alloc([S], BF16)
            t3b = T3.bitcast(BF16)
            qb = t3b[:, 0:S]
            kb = t3b[:, S:2 * S]
            qkm = Rot([((qa, [Buf("mqa%d" % tb) for tb in range(NTB)]), (ka, [Buf("mka%d" % tb) for tb in range(NTB)]), None),
                       ((qb, [Buf("mqb%d" % tb) for tb in range(NTB)]), (kb, [Buf("mkb%d" % tb) for tb in range(NTB)]), BT3)])
            sga = AR.alloc([S], BF16)
            sgb = wgp.rearrange("p a b -> p (a b)")
            sgs = Rot([(sga, [Buf("sga%d" % tb) for tb in range(NTB)], None), (sgb, [Buf("sgb%d" % tb) for tb in range(NTB)], [Bwgp])])
            pre = AR.alloc([S], BF16)
            Bpre = [Buf("pre%d" % tb) for tb in range(NTB)]
            yc = AR.alloc([S], F32)
            Byc = Buf("yc")
            Ws = Rot([(AR.alloc([512], F32), Buf("W%d" % i)) for i in range(3)])
            Pms = Rot([(AR.alloc([512], BF16), Buf("Pm%d" % i)) for i in range(5)])
            dab = AR.alloc([512], F32)
            Bdab = Buf("dab")
            hTt = AR.alloc([512], F32)
            BhTt = Buf("hTt")
            sq = AR.alloc([512], BF16)
            Bsq = Buf("sq")
            rsd = AR.alloc([512], F32)
            Brsd = Buf("rsd")
            gcol = AR.alloc([64], F32)
            Bgcol = Buf("gcol")
            nbias = AR.alloc([2], F32)
            Bnb = Buf("nbias")

            def conv_silu(cc, dst, Bdst, extra_w):
                def w(j):
                    return cv[:, CV_WCONV + cc * 4 + j:CV_WCONV + cc * 4 + j + 1]

                def th1():
                    P.dve(lambda e: e.tensor_scalar(out=yc, in0=pre, scalar1=w(3), scalar2=None, op0=ALU.mult), reads=Bpre + [Bcv], writes=[Byc])
                    P.dve(lambda e: e.scalar_tensor_tensor(out=yc[:, 1:S], in0=pre[:, 0:S - 1], scalar=w(2), in1=yc[:, 1:S],
                                                           op0=ALU.mult, op1=ALU.add), reads=Bpre + [Bcv, Byc], writes=[Byc])

                def th2():
                    for sh, j in ((2, 1), (3, 0)):
                        P.dve(lambda e, sh=sh, j=j: e.scalar_tensor_tensor(out=yc[:, sh:S], in0=pre[:, 0:S - sh], scalar=w(j), in1=yc[:, sh:S],
                                                                           op0=ALU.mult, op1=ALU.add), reads=Bpre + [Bcv, Byc], writes=[Byc])

                def th3():
                    for tb in range(NTB):
                        sl = slice(tb * 512, (tb + 1) * 512)
                        P.act(lambda e, sl=sl: e.activation(out=dst[:, sl], in_=yc[:, sl], func=AF.Silu, bias=cv[:, CV_BCONV + cc:CV_BCONV + cc + 1]),
                              reads=[Byc, Bcv], writes=[Bdst[tb]] + (extra_w if extra_w else []))
                bg.add(th1)
                bg.add(th2)
                bg.add(th3)

            def evac_pre(tb, pp, Bpp):
                P.act(lambda e: e.copy(out=pre[:, tb * 512:(tb + 1) * 512], in_=pp[:, :]), reads=[Bpp], writes=[Bpre[tb]])

            def ml_sched_proj(h):
                (qT, BqT), (kT, BkT), extra = qkm.next()
                sg, Bsg, extra_s = sgs.next()
                v, Bv = vsl.next()

                def evac_sig(tb, pp, Bpp):
                    P.act(lambda e: e.activation(out=sg[:, tb * 512:(tb + 1) * 512], in_=pp[:, :], func=AF.Sigmoid), reads=[Bpp],
                          writes=[Bsg[tb]] + (extra_s if extra_s else []))
                proj_fm(1536 + h * 128, evac_pre)
                conv_silu(h, qT, BqT, extra)
                proj_fm(2048 + h * 128, evac_pre)
                conv_silu(4 + h, kT, BkT, extra)
                proj_tm(2560 + h * 128, v, Bv)
                proj_fm(3072 + h * 128, evac_sig)
                return qT, BqT, kT, BkT, v, Bv, sg, Bsg

            ml_left = [52]
            nxt = ml_sched_proj(0)

            wdma(wgp, Wd["wgp"][l].rearrange("(c p) n -> p c n", p=128), Bwgp)
            P.dve(lambda e: e.tensor_scalar(out=nbias[:, 0:1], in0=cv[:, CV_BF:CV_BF + 1], scalar1=-1.0, scalar2=None, op0=ALU.mult),
                  reads=[Bcv], writes=[Bnb])
            gi_ps = [(pbank[0], Bpb[0]), (pbank[1], Bpb[1]), (pbank[2], Bpb[2]), (pbank[3], Bpb[3])]
            for tb in range(NTB):
                pp, Bpp = prj.next()
                for k in range(KC):
                    P.pe(lambda e, k=k, pp=pp, tb=tb: e.matmul(pp[:, :], lhsT=wgp[:, k, 128:256], rhs=xnT[:, k, tb * 512:(tb + 1) * 512],
                                                              start=(k == 0), stop=(k == KC - 1)), reads=[Bwgp] + xT_bufs(tb), writes=[Bpp])
                P.act(lambda e, pp=pp, tb=tb: e.activation(out=T1[:, tb * 512:(tb + 1) * 512], in_=pp[:, :], func=AF.Exp, scale=-1.0,
                                                           bias=nbias[:, 0:1]), reads=[Bpp, Bnb], writes=[BT1[tb]])
                P.act(lambda e, tb=tb: e.activation(out=T1[:, tb * 512:(tb + 1) * 512], in_=T1[:, tb * 512:(tb + 1) * 512], func=AF.Ln, bias=1.0),
                      reads=[BT1[tb]], writes=[BT1[tb]])
                gp_, Bgp_ = gi_ps[tb]
                for k in range(KC):
                    P.pe(lambda e, k=k, gp_=gp_, tb=tb: e.matmul(gp_[:, :], lhsT=wgp[:, k, 0:128], rhs=xnT[:, k, tb * 512:(tb + 1) * 512],
                                                                start=(k == 0), stop=(k == KC - 1)), reads=[Bwgp] + xT_bufs(tb), writes=[Bgp_])
            P.dve(lambda e: e.tensor_tensor_scan(out=T2, data0=ones_f[:, 0:1].broadcast_to([128, S]), data1=T1, initial=0.0,
                                                 op0=ALU.mult, op1=ALU.add), reads=BT1 + [Bconst], writes=BT2)
            for tb in range(NTB):
                gp_, Bgp_ = gi_ps[tb]
                P.dve(lambda e, gp_=gp_, tb=tb: e.scalar_tensor_tensor(out=T3[:, tb * 512:(tb + 1) * 512], in0=gp_[:, :],
                                                                       scalar=cv[:, CV_BI:CV_BI + 1], in1=T2[:, tb * 512:(tb + 1) * 512],
                                                                       op0=ALU.add, op1=ALU.add),
                      reads=[Bgp_, Bcv, BT2[tb]], writes=[BT3[tb]])
            bg.step(38)
            gcp, Bgcp = pbank[0], Bpb[0]
            for c in range(NT):
                P.pe(lambda e, c=c: e.matmul(gcp[:, c * 4:(c + 1) * 4], lhsT=T3[:, c * 128:(c + 1) * 128], rhs=selc[:, :], start=True, stop=True),
                     reads=[BT3[c // 4], Bconst], writes=[Bgcp])
            P.act(lambda e: e.copy(out=gcol, in_=gcp[:, 0:64]), reads=[Bgcp], writes=[Bgcol])
            P.dve(lambda e: e.tensor_tensor_scan(out=T1, data0=zeros_f[:, 0:1].broadcast_to([128, S]), data1=T3, initial=0.0,
                                                 op0=ALU.add, op1=ALU.max), reads=BT3 + [Bconst], writes=BT1)
            for tb in range(NTB):
                sl = slice(tb * 512, (tb + 1) * 512)
                P.dve(lambda e, sl=sl: e.tensor_tensor(out=T2[:, sl], in0=T2[:, sl], in1=T1[:, sl], op=ALU.subtract),
                      reads=[BT2[tb], BT1[tb]], writes=[BT2[tb]])
                P.act(lambda e, sl=sl: e.activation(out=T2[:, sl], in_=T2[:, sl], func=AF.Exp), reads=[BT2[tb]], writes=[BT2[tb]])
                P.dve(lambda e, sl=sl: e.tensor_scalar(out=T1[:, sl], in0=T1[:, sl], scalar1=-1.0, scalar2=None, op0=ALU.mult),
                      reads=[BT1[tb]], writes=[BT1[tb]])

            sps = Rot([(pbank[0], Bpb[0]), (pbank[1], Bpb[1]), (pbank[5], Bpb[5])])
            ngp, Bngp = pbank[2], Bpb[2]
            nump, Bnump = pbank[3], Bpb[3]
            denp, Bdenp = pbank[4], Bpb[4]
            QSCALE = 128.0 ** -0.5

            def ml_attention(h, qT, BqT, kT, BkT, v, Bv, sg, Bsg, mix, Bmix):
                for tb in range(NTB):
                    tsl = slice(tb * 512, (tb + 1) * 512)
                    P.pe(lambda e: e.matmul(ngp[:, :], lhsT=selh[h][:, :], rhs=T1[:, tsl], start=True, stop=True),
                         reads=[Bconst, BT1[tb]], writes=[Bngp])
                    nC = 4 * tb + 4
                    state = [None] * nC

                    def stage0(c):
                        r = c - 4 * tb
                        off = 128 * r if r >= 0 else 0
                        sp_, Bsp_ = sps.next()
                        W_, BW = Ws.next()
                        Pm, BPm = Pms.next()
                        P.pe(lambda e: e.matmul(sp_[:, off:512], lhsT=kT[:, c * 128:(c + 1) * 128],
                                                rhs=qT[:, tb * 512 + off:(tb + 1) * 512], start=True, stop=True),
                             reads=[BkT[c // 4], BqT[tb]], writes=[Bsp_])
                        P.act(lambda e: e.activation(out=W_[:, off:512], in_=ngp[:, off:512], func=AF.Exp,
                                                     bias=gcol[:, c * 4 + h:c * 4 + h + 1]),
                              reads=[Bngp, Bgcol], writes=[BW])
                        P.dve(lambda e: e.scalar_tensor_tensor(out=Pm[:, off:512], in0=sp_[:, off:512], scalar=QSCALE,
                                                               in1=W_[:, off:512], op0=ALU.mult, op1=ALU.mult),
                              reads=[Bsp_, BW], writes=[BPm])
                        if r >= 0:
                            P.dve(lambda e: e.tensor_tensor(out=Pm[:, off:off + 128], in0=Pm[:, off:off + 128], in1=m_incl[:, :], op=ALU.mult),
                                  reads=[BPm, Bconst], writes=[BPm])
                        state[c] = (off, Pm, BPm)

                    def stage1(c):
                        off, Pm, BPm = state[c]
                        P.pe(lambda e: e.matmul(nump[:, off:512], lhsT=v[:, c, :], rhs=Pm[:, off:512], start=(c == 0), stop=(c == nC - 1)),
                             reads=[Bv, BPm], writes=[Bnump])
                        P.pe(lambda e: e.matmul(denp[:, off:512], lhsT=ones_bf[:, :], rhs=Pm[:, off:512], start=(c == 0), stop=(c == nC - 1)),
                             reads=[Bconst, BPm], writes=[Bdenp])

                    for step in range(nC + 3):
                        if step < nC:
                            stage0(step)
                        if 0 <= step - 3 < nC:
                            stage1(step - 3)
                        ml_left[0] -= 1
                        bg.step_auto(ml_left[0])

                    ep, Bep = sps.next()
                    P.pe(lambda e: e.matmul(ep[:, :], lhsT=selh[h][:, :], rhs=T2[:, tsl], start=True, stop=True),
                         reads=[Bconst, BT2[tb]], writes=[Bep])
                    P.act(lambda e: e.activation(out=dab, in_=denp[:, :], func=AF.Abs), reads=[Bdenp], writes=[Bdab])
                    P.dve(lambda e: e.tensor_tensor(out=dab, in0=dab, in1=ep[:, :], op=ALU.max), reads=[Bdab, Bep], writes=[Bdab])
                    P.act(lambda e: e.activation(out=dab, in_=dab, func=AF.Ln), reads=[Bdab], writes=[Bdab])
                    P.act(lambda e: e.activation(out=dab, in_=dab, func=AF.Exp, scale=-1.0), reads=[Bdab], writes=[Bdab])
                    P.dve(lambda e: e.tensor_tensor(out=hTt, in0=nump[:, :], in1=dab, op=ALU.mult), reads=[Bnump, Bdab], writes=[BhTt])
                    P.act(lambda e: e.activation(out=sq, in_=hTt, func=AF.Square), reads=[BhTt], writes=[Bsq])

                    def postB(tb=tb, tsl=tsl):
                        mp, Bmp = sps.next()
                        P.pe(lambda e: e.matmul(mp[:, :], lhsT=ones_bf[:, :], rhs=sq, start=True, stop=True), reads=[Bconst, Bsq], writes=[Bmp])
                        rstd_from_ms(rsd, mp[:, :], [Bmp], [Brsd], scale=1.0 / 128.0)
                        P.dve(lambda e: e.tensor_tensor(out=hTt, in0=hTt, in1=rsd, op=ALU.mult), reads=[BhTt, Brsd], writes=[BhTt])
                        P.dve(lambda e: e.scalar_tensor_tensor(out=mix[:, tsl], in0=hTt, scalar=cv[:, CV_GHEAD + h:CV_GHEAD + h + 1],
                                                               in1=sg[:, tsl], op0=ALU.mult, op1=ALU.mult),
                              reads=[BhTt, Bcv, Bsg[tb]], writes=[Bmix[tb]])
                    bg.add_timed(3, postB)

            bg.flush()
            for h in range(4):
                cur = nxt
                if h + 1 < 4:
                    nxt = ml_sched_proj(h + 1)
                mix, Bmix = mixs.next()
                ml_left[0] = 52
                ml_attention(h, *cur, mix, Bmix)
                bg.flush()
                out_proj(4 + h, mix, Bmix)
            bg.flush()


        def xattn(l):
            P.barrier()
            AR.reset()
            norm_T(Wd["g_xattn"][l:l + 1, :], x_tile, x_bufs, NT, xnT, BxT)
            P.barrier()
            AR.reset()
            cv = AR.alloc([NCV], F32)
            Bcv = Buf("cvx")
            sdma(cv, Wd["cvec"][l], Bcv)
            knT = AR.alloc([KC, MEM], BF16)
            BknT = [Buf("knT%d" % c) for c in range(KC)]
            vx = AR.alloc([2, D], BF16)
            Bvx = Buf("vx")
            keep = AR.off
            memx = AR.alloc([2, D], F32)
            Bmem = [Buf("mem%d" % i) for i in range(2)]
            memnT = AR.alloc([KC, MEM], BF16)
            BmT = [Buf("memnT%d" % i) for i in range(2)]
            wk = AR.alloc([KC, D], BF16)
            Bwk = Buf("wk")
            wv = AR.alloc([KC, D], BF16)
            Bwv = Buf("wv")
            kf = [AR.alloc([MEM], F32) for _ in range(2)]
            Bkf = [Buf("kf%d" % i) for i in range(2)]
            ksq = [AR.alloc([MEM], BF16) for _ in range(2)]
            Bksq = [Buf("ksq%d" % i) for i in range(2)]
            krs = AR.alloc([MEM], F32)
            Bkrs = Buf("krs")
            sdma(memx, mem_d.rearrange("(t p) d -> p t d", p=128), Bmem[0])
            wdma(wk, Wd["w_xk"][l].rearrange("(c p) n -> p c n", p=128), Bwk)
            wdma(wv, Wd["w_xv"][l].rearrange("(c p) n -> p c n", p=128), Bwv)
            norm_T(Wd["g_mem"][l:l + 1, :], lambda tt: memx[:, tt, :], lambda tt: [Bmem[0]], 2, memnT, BmT)
            sps = Rot([(pbank[0], Bpb[0]), (pbank[1], Bpb[1])])
            ms_ps, Bms = pbank[2], Bpb[2]
            for h in range(4):
                for fc in range(2):
                    f = 2 * h + fc
                    kp, Bkp = sps.next()
                    for k in range(KC):
                        P.pe(lambda e, k=k, f=f, kp=kp: e.matmul(kp[:, 0:MEM], lhsT=wk[:, k, f * 128:(f + 1) * 128], rhs=memnT[:, k, :],
                                                                start=(k == 0), stop=(k == KC - 1)), reads=[Bwk] + BmT, writes=[Bkp])
                    P.act(lambda e, fc=fc, kp=kp: e.copy(out=kf[fc], in_=kp[:, 0:MEM]), reads=[Bkp], writes=[Bkf[fc]])
                    P.act(lambda e, fc=fc, kp=kp: e.activation(out=ksq[fc], in_=kp[:, 0:MEM], func=AF.Square), reads=[Bkp], writes=[Bksq[fc]])
                for fc in range(2):
                    P.pe(lambda e, fc=fc: e.matmul(ms_ps[:, 0:MEM], lhsT=ones_bf[:, :], rhs=ksq[fc], start=(fc == 0), stop=(fc == 1)),
                         reads=[Bconst, Bksq[fc]], writes=[Bms])
                rstd_from_ms(krs, ms_ps[:, 0:MEM], [Bms], [Bkrs], scale=1.0 / 256.0)
                for fc in range(2):
                    f = 2 * h + fc
                    P.dve(lambda e, fc=fc, f=f: e.scalar_tensor_tensor(out=knT[:, f, :], in0=kf[fc], scalar=cv[:, CV_GK + fc:CV_GK + fc + 1],
                                                                      in1=krs, op0=ALU.mult, op1=ALU.mult),
                          reads=[Bkf[fc], Bcv, Bkrs], writes=[BknT[f]])
            for mt in range(2):
                for dh in range(2):
                    vp, Bvp = sps.next()
                    for k in range(KC):
                        P.pe(lambda e, k=k, mt=mt, dh=dh, vp=vp: e.matmul(vp[:, :], lhsT=memnT[:, k, mt * 128:(mt + 1) * 128],
                                                                         rhs=wv[:, k, dh * 512:(dh + 1) * 512], start=(k == 0), stop=(k == KC - 1)),
                             reads=[Bwv] + BmT, writes=[Bvp])
                    P.act(lambda e, mt=mt, dh=dh, vp=vp: e.copy(out=vx[:, mt, dh * 512:(dh + 1) * 512], in_=vp[:, :]), reads=[Bvp], writes=[Bvx])
            P.barrier()
            AR.reset(keep)
            wq = AR.alloc([KC, D], BF16)
            Bwq = Buf("wq")
            wo = AR.alloc([KC, D], BF16)
            Bwo = Buf("wo")
            wdma(wq, Wd["w_xq"][l].rearrange("(c p) n -> p c n", p=128), Bwq)
            wdma(wo, Wd["w_xo"][l].rearrange("(c p) n -> p c n", p=128), Bwo)

            def qset(i):
                return dict(qf=[AR.alloc([512], F32) for _ in range(2)], Bqf=[Buf("qf%d_%d" % (i, j)) for j in range(2)],
                            qsq=[AR.alloc([512], BF16) for _ in range(2)], Bqsq=[Buf("qsq%d_%d" % (i, j)) for j in range(2)],
                            qn=[AR.alloc([512], BF16) for _ in range(2)], Bqn=[Buf("qn%d_%d" % (i, j)) for j in range(2)],
                            qrs=AR.alloc([512], F32), Bqrs=Buf("qrs%d" % i))
            qsets = Rot([qset(i) for i in range(2)])
            Exs = Rot([([AR.alloc([512], BF16) for _ in range(2)], [Buf("Ex%d_%d" % (i, j)) for j in range(2)]) for i in range(2)])
            recs = Rot([(AR.alloc([512], F32), Buf("rec%d" % i)) for i in range(2)])
            aTs = Rot([(AR.alloc([KC, 512], BF16), [Buf("aT%d_%d" % (i, c)) for c in range(KC)]) for i in range(2)])
            qps = Rot([(pbank[0], Bpb[0]), (pbank[1], Bpb[1])])
            scps = Rot([(pbank[2], Bpb[2]), (pbank[3], Bpb[3])])
            dnp, Bdnp = pbank[4], Bpb[4]
            ops2 = Rot([(pbank[6], Bpb[6]), (pbank[7], Bpb[7])])
            yps = Rot([(pbank[5], Bpb[5])])

            def stageA(tb, h):
                tsl = slice(tb * 512, (tb + 1) * 512)
                qs = qsets.next()
                for fc in range(2):
                    f = 2 * h + fc
                    qp, Bqp = qps.next()
                    for k in range(KC):
                        P.pe(lambda e, k=k: e.matmul(qp[:, :], lhsT=wq[:, k, f * 128:(f + 1) * 128], rhs=xnT[:, k, tsl],
                                                     start=(k == 0), stop=(k == KC - 1)), reads=[Bwq] + xT_bufs(tb), writes=[Bqp])
                    P.act(lambda e: e.copy(out=qs["qf"][fc], in_=qp[:, :]), reads=[Bqp], writes=[qs["Bqf"][fc]])
                    P.act(lambda e: e.activation(out=qs["qsq"][fc], in_=qp[:, :], func=AF.Square), reads=[Bqp], writes=[qs["Bqsq"][fc]])
                mp, Bmp = qps.next()
                for fc in range(2):
                    P.pe(lambda e, fc=fc: e.matmul(mp[:, :], lhsT=ones_bf[:, :], rhs=qs["qsq"][fc], start=(fc == 0), stop=(fc == 1)),
                         reads=[Bconst, qs["Bqsq"][fc]], writes=[Bmp])
                rstd_from_ms(qs["qrs"], mp[:, :], [Bmp], [qs["Bqrs"]], scale=1.0 / 256.0)
                for fc in range(2):
                    P.dve(lambda e, fc=fc: e.scalar_tensor_tensor(out=qs["qn"][fc], in0=qs["qf"][fc], scalar=cv[:, CV_GQ + fc:CV_GQ + fc + 1],
                                                                 in1=qs["qrs"], op0=ALU.mult, op1=ALU.mult),
                          reads=[qs["Bqf"][fc], Bcv, qs["Bqrs"]], writes=[qs["Bqn"][fc]])
                return qs

            def stageB(tb, h, qs, attnT, BaT):
                Ex, BEx = Exs.next()
                rec, Brec = recs.next()
                for mc in range(2):
                    sp_, Bsp_ = scps.next()
                    for fc in range(2):
                        P.pe(lambda e, fc=fc: e.matmul(sp_[:, :], lhsT=knT[:, 2 * h + fc, mc * 128:(mc + 1) * 128], rhs=qs["qn"][fc],
                                                       start=(fc == 0), stop=(fc == 1)), reads=[BknT[2 * h + fc], qs["Bqn"][fc]], writes=[Bsp_])
                    P.act(lambda e: e.activation(out=Ex[mc], in_=sp_[:, :], func=AF.Exp, scale=1.0 / 16.0), reads=[Bsp_], writes=[BEx[mc]])
                for mc in range(2):
                    P.pe(lambda e, mc=mc: e.matmul(dnp[:, :], lhsT=ones_bf[:, :], rhs=Ex[mc], start=(mc == 0), stop=(mc == 1)),
                         reads=[Bconst, BEx[mc]], writes=[Bdnp])
                P.act(lambda e: e.activation(out=rec, in_=dnp[:, :], func=AF.Ln), reads=[Bdnp], writes=[Brec])
                P.act(lambda e: e.activation(out=rec, in_=rec, func=AF.Exp, scale=-1.0), reads=[Brec], writes=[Brec])
                for dc in range(2):
                    op2, Bop2 = ops2.next()
                    for mc in range(2):
                        P.pe(lambda e, mc=mc: e.matmul(op2[:, :], lhsT=vx[:, mc, h * 256 + dc * 128:h * 256 + (dc + 1) * 128], rhs=Ex[mc],
                                                       start=(mc == 0), stop=(mc == 1)), reads=[Bvx, BEx[mc]], writes=[Bop2])
                    P.dve(lambda e: e.tensor_tensor(out=attnT[:, 2 * h + dc, :], in0=op2[:, :], in1=rec, op=ALU.mult),
                          reads=[Bop2, Brec], writes=[BaT[2 * h + dc]])

            def stageO(tb, attnT, BaT):
                for i in range(4):
                    tt = 4 * tb + i
                    for dh in range(2):
                        yp, Byp = yps.next()
                        for k in range(KC):
                            P.pe(lambda e, k=k: e.matmul(yp[:, :], lhsT=attnT[:, k, i * 128:(i + 1) * 128], rhs=wo[:, k, dh * 512:(dh + 1) * 512],
                                                         start=(k == 0), stop=(k == KC - 1)), reads=[BaT[k], Bwo], writes=[Byp])
                        P.dve(lambda e: e.tensor_tensor(out=x_sb[:, tt, dh * 512:(dh + 1) * 512], in0=yp[:, :],
                                                        in1=x_sb[:, tt, dh * 512:(dh + 1) * 512], op=ALU.add),
                              reads=[Byp, Bx[tt][dh]], writes=[Bx[tt][dh]])

            pending = None
            pendO = None
            for tb in range(NTB):
                aT = aTs.next()
                for h in range(4):
                    qs = stageA(tb, h)
                    if pending is not None:
                        stageB(*pending)
                    if pendO is not None and h == 1:
                        stageO(*pendO)
                        pendO = None
                    pending = (tb, h, qs, aT[0], aT[1])
                pendO_next = (tb, aT[0], aT[1])
                if pendO is not None:
                    stageO(*pendO)
                pendO = pendO_next
            stageB(*pending)
            stageO(*pendO)


        done = False
        for l in range(L):
            ffn(l, "g_ff1", "w_ff1_gate", "w_ff1_up", "w_ff1_down")
            if stop_after == "ffn1":
                break
            mixer(l)
            if stop_after in ("mix", "skip_sb"):
                break
            xattn(l)
            if stop_after == "xattn":
                break
            ffn(l, "g_ff2", "w_ff2_gate", "w_ff2_up", "w_ff2_down")
        P.barrier()
        for c in range(4):
            P.dma("sp", lambda e, c=c: e.dma_start(out=ov[:, 4 * c:4 * c + 4, :], in_=x_sb[:, 4 * c:4 * c + 4, :]),
                  reads=[Bx[t][h] for t in range(4 * c, 4 * c + 4) for h in range(2)], sem_buf=Bout)
        P.finalize(final_wait_bufs=[Bout])
    return nc


def _layer_params(inp, l0, l1):
    f = lambda a: np.ascontiguousarray(np.asarray(a, dtype=np.float32))
    Ln = l1 - l0
    d = {}
    for name, _ in PARAMS:
        if name in ("wgp", "cvec"):
            continue
        d[name] = f(np.asarray(inp[name])[l0:l1])
    w_in = np.asarray(inp["w_in"])[l0:l1]
    wgp = np.zeros((Ln, D, 256), np.float32)
    for h in range(4):
        wgp[:, :, 32 * h] = w_in[:, :, 3584 + h]
        wgp[:, :, 128 + 32 * h] = w_in[:, :, 3588 + h]
    d["wgp"] = wgp
    cv = np.zeros((Ln, 128, NCV), np.float32)
    b_conv = np.asarray(inp["b_conv"])[l0:l1]
    w_conv = np.asarray(inp["w_conv"])[l0:l1]
    g_head = np.asarray(inp["g_mlstm_head"])[l0:l1]
    g_q = np.asarray(inp["g_qnorm"])[l0:l1]
    g_k = np.asarray(inp["g_knorm"])[l0:l1]
    b_gate = np.asarray(inp["b_gate"])[l0:l1]
    cv[:, :, CV_BCONV:CV_BCONV + 8] = b_conv.reshape(Ln, 8, 128).transpose(0, 2, 1)
    cv[:, :, CV_WCONV:CV_WCONV + 32] = w_conv.reshape(Ln, 4, 8, 128).transpose(0, 3, 2, 1).reshape(Ln, 128, 32)
    cv[:, :, CV_GHEAD:CV_GHEAD + 4] = g_head.transpose(0, 2, 1)
    cv[:, :, CV_GQ:CV_GQ + 2] = g_q.reshape(Ln, 2, 128).transpose(0, 2, 1)
    cv[:, :, CV_GK:CV_GK + 2] = g_k.reshape(Ln, 2, 128).transpose(0, 2, 1)
    for h in range(4):
        cv[:, 32 * h, CV_BI] = b_gate[:, h]
        cv[:, 32 * h, CV_BF] = b_gate[:, 4 + h]
    d["cvec"] = cv
    return d


_PROG_CACHE = {}
FUSED = True


def _get_prog(L, stop_after=None):
    key = (L, stop_after)
    if key not in _PROG_CACHE:
        _PROG_CACHE[key] = build_program(L, stop_after)
    return _PROG_CACHE[key]


def run_layers(x, mem, inp, l0, l1, stop_after=None):
    params = _layer_params(inp, l0, l1)
    nc = _get_prog(l1 - l0, stop_after)
    in_maps = []
    for b in range(NCORES):
        m = {"x": np.ascontiguousarray(x[b]), "mem": np.ascontiguousarray(mem[b])}
        m.update(params)
        in_maps.append(m)
    res = run_bass_kernel_spmd(nc, in_maps, core_ids=list(range(NCORES)))
    return np.stack([np.asarray(r["out"]) for r in res.results], axis=0)


def kernel(**inputs):
    x = np.asarray(inputs["x"], dtype=np.float32)
    mem = np.asarray(inputs["mem"], dtype=np.float32)
    if FUSED:
        return run_layers(x, mem, inputs, 0, DEPTH).astype(np.float32)
    for l in range(DEPTH):
        x = run_layers(x, mem, inputs, l, l + 1)
    return x.astype(np.float32)
```

```python
import contextlib
import types
import numpy as np
import concourse.bass as bass
import concourse.mybir as mybir
from concourse.bass_utils import run_bass_kernel_spmd

F32 = mybir.dt.float32
BF16 = mybir.dt.bfloat16
ALU = mybir.AluOpType
AF = mybir.ActivationFunctionType

S = 2048
D = 1024
NT = 16
NTB = 4
KC = 8
DFF = 2816
NIN = 3592
MEM = 256
EPS = 1e-6
DEPTH = 2
NCORES = 8
NCV = 64

QUEUES = ("pe", "act", "dve", "pool", "sp")


class Buf:
    __slots__ = ("name", "lw", "lr", "sem", "semcount")

    def __init__(self, name=""):
        self.name = name
        self.lw = {}
        self.lr = {}
        self.sem = None
        self.semcount = 0


class Op:
    __slots__ = ("q", "fn", "deps", "signal", "sem", "val", "dma", "key", "idx")


def _freeze(fn):
    if fn.__closure__ is None:
        return fn
    cells = []
    for c in fn.__closure__:
        try:
            cells.append(types.CellType(c.cell_contents))
        except ValueError:
            cells.append(c)
    return types.FunctionType(fn.__code__, fn.__globals__, fn.__name__, fn.__defaults__, tuple(cells))


class Prog:
    def __init__(self, nc, stack):
        self.nc = nc
        self.stack = stack
        self.ops = []
        self.byq = {q: [] for q in QUEUES}
        self.nsem = 0
        self._dmakey = 0
        self.last = {}
        self.pending_barrier = {}

    def new_sem(self, name):
        self.nsem += 1
        return self.stack.enter_context(self.nc.semaphore("s%d_%s" % (self.nsem, name)))

    def barrier(self):
        lasts = set(self.last.values())
        for q in QUEUES:
            self.pending_barrier[q] = set(lasts)

    def add(self, q, fn, reads=(), writes=(), dma=False, sem_buf=None):
        op = Op()
        op.q = q
        op.fn = _freeze(fn)
        op.dma = dma
        op.signal = dma
        op.sem = None
        op.val = 0
        op.idx = len(self.ops)
        if dma:
            self._dmakey += 1
            op.key = ("dma", self._dmakey)
        else:
            op.key = q
        deps = set()
        for b in reads:
            for k, w in b.lw.items():
                deps.add(w)
        for b in writes:
            for k, r in b.lr.items():
                if dma or r.dma or k != op.key:
                    deps.add(r)
            for k, w in b.lw.items():
                if dma or w.dma or k != op.key:
                    deps.add(w)
        pb = self.pending_barrier.pop(q, None)
        if pb:
            deps |= pb
        op.deps = deps
        for b in reads:
            b.lr[op.key] = op
        for b in writes:
            b.lw = {op.key: op}
            b.lr = {}
        if dma:
            sb = sem_buf if sem_buf is not None else writes[0]
            if sb.sem is None:
                sb.sem = self.new_sem("d")
            sb.semcount += 16
            op.sem = sb.sem
            op.val = sb.semcount
        self.ops.append(op)
        self.byq[q].append(op)
        self.last[op.key if not dma else ("dmaq", q, id(op.sem))] = op
        return op

    def pe(self, fn, reads=(), writes=()):
        return self.add("pe", fn, reads, writes)

    def act(self, fn, reads=(), writes=()):
        return self.add("act", fn, reads, writes)

    def dve(self, fn, reads=(), writes=()):
        return self.add("dve", fn, reads, writes)

    def pool(self, fn, reads=(), writes=()):
        return self.add("pool", fn, reads, writes)

    def dma(self, q, fn, reads=(), writes=(), sem_buf=None):
        return self.add(q, fn, reads, writes, dma=True, sem_buf=sem_buf)

    def finalize(self, final_wait_bufs=()):
        nc = self.nc
        for op in self.ops:
            for d in op.deps:
                d.signal = True
        LIM = 30000
        for q in QUEUES:
            sem = None
            cnt = 0
            for op in self.byq[q]:
                if op.dma or not op.signal:
                    continue
                if sem is None or cnt >= LIM:
                    sem = self.new_sem("c_" + q)
                    cnt = 0
                cnt += 1
                op.sem = sem
                op.val = cnt
        finals = [(b.sem, b.semcount) for b in final_wait_bufs]

        def run_queue(q, eng):
            waited = {}
            for op in self.byq[q]:
                need = {}
                for d in op.deps:
                    s = d.sem
                    cur = need.get(id(s))
                    if cur is None or cur[1] < d.val:
                        need[id(s)] = (s, d.val)
                for sid, (s, v) in need.items():
                    if waited.get(sid, 0) < v:
                        eng.wait_ge(s, v)
                        waited[sid] = v
                ins = op.fn(eng)
                if op.signal:
                    ins.then_inc(op.sem, 16 if op.dma else 1)
            if q == "sp":
                for s, v in finals:
                    eng.wait_ge(s, v)

        with nc.Block() as block:
            @block.tensor
            def _(e):
                run_queue("pe", e)

            @block.scalar
            def _(e):
                run_queue("act", e)

            @block.vector
            def _(e):
                run_queue("dve", e)

            @block.gpsimd
            def _(e):
                run_queue("pool", e)

            @block.sync
            def _(e):
                run_queue("sp", e)


class Arena:
    def __init__(self, tensor, nbytes):
        self.t = tensor
        self.n = nbytes
        self.off = 0

    def reset(self, off=0):
        self.off = off

    def alloc(self, shape, dtype):
        esz = 4 if dtype == F32 else 2
        n = 1
        for s in shape:
            n *= s
        nb = (n * esz + 31) // 32 * 32
        assert self.off + nb <= self.n, ("arena overflow", self.off, nb, self.n)
        v = self.t[:, self.off // 2:(self.off + n * esz) // 2]
        self.off += nb
        if dtype == F32:
            v = v.bitcast(F32)
        if len(shape) == 2:
            v = v.rearrange("p (a b) -> p a b", a=shape[0])
        elif len(shape) == 3:
            v = v.rearrange("p (a b c) -> p a b c", a=shape[0], b=shape[1])
        return v


class Rot:
    def __init__(self, items):
        self.items = items
        self.i = 0

    def next(self):
        it = self.items[self.i % len(self.items)]
        self.i += 1
        return it


PARAMS = [
    ("g_ff1", [D]), ("w_ff1_gate", [D, DFF]), ("w_ff1_up", [D, DFF]), ("w_ff1_down", [DFF, D]),
    ("g_mix", [D]), ("w_in", [D, NIN]), ("wgp", [D, 256]), ("cvec", [128, NCV]),
    ("w_out", [D, D]), ("g_xattn", [D]), ("g_mem", [D]), ("w_xq", [D, D]), ("w_xk", [D, D]),
    ("w_xv", [D, D]), ("w_xo", [D, D]), ("g_ff2", [D]), ("w_ff2_gate", [D, DFF]),
    ("w_ff2_up", [D, DFF]), ("w_ff2_down", [DFF, D]),
]
CV_BCONV = 0
CV_WCONV = 8
CV_GHEAD = 40
CV_GQ = 44
CV_GK = 46
CV_BI = 48
CV_BF = 49


def build_program(L, stop_after=None):
    nc = bass.Bass("TRN2", target_bir_lowering=False)

    def din(name, shape):
        return nc.dram_tensor(name, shape, F32, kind="ExternalInput").ap()

    x_d = din("x", [S, D])
    mem_d = din("mem", [MEM, D])
    Wd = {name: din(name, [L] + shape) for name, shape in PARAMS}
    out_d = nc.dram_tensor("out", [S, D], F32, kind="ExternalOutput").ap()

    with contextlib.ExitStack() as st:
        P = Prog(nc, st)

        def sb(name, shape, dt):
            return st.enter_context(nc.sbuf_tensor(name, shape, dt))

        x_sb = sb("x_sb", [128, NT, D], F32)
        xnT = sb("xnT", [128, KC, S], BF16)
        ARENA_BYTES = 104 * 1024
        arena_t = sb("arena", [128, ARENA_BYTES // 2], BF16)
        AR = Arena(arena_t, ARENA_BYTES)
        ident = sb("ident", [128, 128], BF16)
        m_strict = sb("m_strict", [128, 128], BF16)
        m_incl = sb("m_incl", [128, 128], BF16)
        l_incl = sb("l_incl", [128, 128], BF16)
        ones_bf = sb("ones_bf", [128, 128], BF16)
        zeros_bf = sb("zeros_bf", [128, 128], BF16)
        ones_f = sb("ones_f", [128, 512], F32)
        zeros_f = sb("zeros_f", [128, 1], F32)
        eps_t = sb("eps_t", [128, 1], F32)
        selc = sb("selc", [128, 4], F32)
        selh = [sb("selh%d" % h, [128, 128], F32) for h in range(4)]
        cscr = sb("cscr", [128, 128], F32)
        Bconst = Buf("const")
        Bcscr = Buf("cscr")

        pbank = [st.enter_context(nc.psum_tensor("pb%d" % i, [128, 512], F32)) for i in range(8)]
        Bpb = [Buf("pb%d" % i) for i in range(8)]

        Bx = [[Buf("x%d_%d" % (t, h)) for h in range(2)] for t in range(NT)]
        BxT = [Buf("xnT%d" % t) for t in range(NT)]
        Bout = Buf("out")

        def mk_mask(dst, cm, pat, op):
            P.pool(lambda e: e.memset(cscr[:], 1.0), writes=[Bcscr])
            P.pool(lambda e: e.affine_select(out=cscr[:], in_=cscr[:], pattern=[[pat, 128]], compare_op=op,
                                             fill=0.0, base=0, channel_multiplier=cm), reads=[Bcscr], writes=[Bcscr])
            P.dve(lambda e: e.tensor_copy(out=dst[:], in_=cscr[:]), reads=[Bcscr], writes=[Bconst])

        mk_mask(ident, 1, -1, ALU.is_equal)
        mk_mask(m_strict, -1, 1, ALU.is_gt)
        mk_mask(m_incl, -1, 1, ALU.is_ge)
        mk_mask(l_incl, 1, -1, ALU.is_ge)
        P.dve(lambda e: e.memset(ones_bf[:], 1.0), writes=[Bconst])
        P.dve(lambda e: e.memset(zeros_bf[:], 0.0), writes=[Bconst])
        P.dve(lambda e: e.memset(ones_f[:], 1.0), writes=[Bconst])
        P.dve(lambda e: e.memset(zeros_f[:], 0.0), writes=[Bconst])
        P.dve(lambda e: e.memset(eps_t[:], EPS), writes=[Bconst])
        P.pool(lambda e: e.memset(selc[:], 1.0), writes=[Bconst])
        P.pool(lambda e: e.affine_select(out=selc[:], in_=selc[:], pattern=[[-32, 4]], compare_op=ALU.is_equal,
                                         fill=0.0, base=0, channel_multiplier=1), reads=[Bconst], writes=[Bconst])
        for h in range(4):
            P.pool(lambda e, h=h: e.memset(selh[h][:], 1.0), writes=[Bconst])
            P.pool(lambda e, h=h: e.affine_select(out=selh[h][:], in_=selh[h][:], pattern=[[0, 128]], compare_op=ALU.is_equal,
                                                  fill=0.0, base=-32 * h, channel_multiplier=1), reads=[Bconst], writes=[Bconst])

        xv = x_d.rearrange("(t p) d -> p t d", p=128)
        ov = out_d.rearrange("(t p) d -> p t d", p=128)
        for c in range(4):
            bl = Buf("xload%d" % c)
            P.dma("sp", lambda e, c=c: e.dma_start(out=x_sb[:, 4 * c:4 * c + 4, :], in_=xv[:, 4 * c:4 * c + 4, :]),
                  writes=[Bx[t][h] for t in range(4 * c, 4 * c + 4) for h in range(2)], sem_buf=bl)

        def wdma(dst, src, buf):
            P.dma("pool", lambda e: e.dma_start(out=dst, in_=src), writes=[buf])

        def sdma(dst, src, buf):
            P.dma("sp", lambda e: e.dma_start(out=dst, in_=src), writes=[buf])

        def rstd_from_ms(dst, src, rd, wr, scale=1.0):
            P.act(lambda e: e.activation(out=dst, in_=src, func=AF.Ln, bias=eps_t[:, 0:1], scale=scale), reads=rd, writes=wr)
            P.act(lambda e: e.activation(out=dst, in_=dst, func=AF.Exp, scale=-0.5), reads=wr, writes=wr)


        def norm_T(g_row, src_tile, src_bufs, ntile, dstT, dst_bufs, tok0=0):
            mark = AR.off
            gb = AR.alloc([D], F32)
            Bgb = Buf("gb")
            ss = AR.alloc([NT], F32)
            rs = AR.alloc([NT], F32)
            Bss = Buf("ss")
            junk = AR.alloc([D], BF16)
            Bjunk = Buf("junk")
            xns = Rot([(AR.alloc([D], BF16), Buf("xn%d" % i)) for i in range(2)])
            sdma(gb, g_row.partition_broadcast(128), Bgb)
            P.dve(lambda e: e.memset(ss, 0.0), writes=[Bss])
            for tt in range(ntile):
                P.act(lambda e, tt=tt: e.activation(out=junk, in_=src_tile(tt), func=AF.Square, scale=1.0 / 32.0,
                                                    accum_out=ss[:, tt:tt + 1]),
                      reads=src_bufs(tt) + [Bss], writes=[Bjunk, Bss])
            rstd_from_ms(rs[:, 0:ntile], ss[:, 0:ntile], [Bss], [Bss])
            pts = Rot([(pbank[6], Bpb[6]), (pbank[7], Bpb[7])])
            for tt in range(ntile):
                xn, Bxn = xns.next()
                P.dve(lambda e, tt=tt, xn=xn: e.scalar_tensor_tensor(out=xn, in0=src_tile(tt), scalar=rs[:, tt:tt + 1],
                                                                     in1=gb, op0=ALU.mult, op1=ALU.mult),
                      reads=src_bufs(tt) + [Bss, Bgb], writes=[Bxn])
                pt, Bpt = pts.next()
                ptb = pt[:, :].bitcast(BF16)
                for c in range(KC):
                    P.pe(lambda e, c=c, xn=xn, ptb=ptb: e.transpose(ptb[:, c * 128:(c + 1) * 128], xn[:, c * 128:(c + 1) * 128], ident[:]),
                         reads=[Bxn, Bconst], writes=[Bpt])
                P.dve(lambda e, tt=tt, ptb=ptb: e.tensor_copy(out=dstT[:, :, tok0 + tt * 128: tok0 + (tt + 1) * 128],
                                                              in_=ptb.rearrange("p (c t) -> p c t", c=KC)),
                      reads=[Bpt], writes=[dst_bufs[tt]])
            return mark

        def x_tile(tt):
            return x_sb[:, tt, :]

        def x_bufs(tt):
            return [Bx[tt][0], Bx[tt][1]]

        def xT_bufs(tb):
            return [BxT[4 * tb + i] for i in range(4)]

        def ffn(l, g_name, wg_name, wu_name, wd_name):
            P.barrier()
            AR.reset()
            norm_T(Wd[g_name][l:l + 1, :], x_tile, x_bufs, NT, xnT, BxT)
            P.barrier()
            AR.reset()
            BLK = 256
            NB = DFF // BLK
            groups = [[0, 1, 2, 3], [4, 5, 6, 7], [8, 9, 10]]
            GMAX = 8
            hT = AR.alloc([GMAX, S], BF16)
            BhT = [[Buf("hT%d_%d" % (j, tb)) for tb in range(NTB)] for j in range(GMAX)]
            wdg = AR.alloc([GMAX, D], BF16)
            Bwd = [Buf("wd%d" % j) for j in range(GMAX // 2)]
            NSL = 3
            wgs = [(AR.alloc([KC, BLK], BF16), Buf("wg%d" % i)) for i in range(NSL)]
            wus = [(AR.alloc([KC, BLK], BF16), Buf("wu%d" % i)) for i in range(NSL)]
            sgs = Rot([(AR.alloc([512], F32), Buf("sg%d" % i)) for i in range(2)])
            wgv = Wd[wg_name][l].rearrange("(c p) n -> p c n", p=128)
            wuv = Wd[wu_name][l].rearrange("(c p) n -> p c n", p=128)
            wdv = Wd[wd_name][l].rearrange("(c p) d -> p c d", p=128)

            def load_blk(b):
                s = b % NSL
                wdma(wgs[s][0], wgv[:, :, b * BLK:(b + 1) * BLK], wgs[s][1])
                wdma(wus[s][0], wuv[:, :, b * BLK:(b + 1) * BLK], wus[s][1])

            PF = 2
            for b in range(PF):
                load_blk(b)
            gps = Rot([(pbank[0], Bpb[0]), (pbank[1], Bpb[1])])
            ups = Rot([(pbank[2], Bpb[2]), (pbank[3], Bpb[3])])
            yps = Rot([(pbank[4], Bpb[4]), (pbank[5], Bpb[5])])
            for grp in groups:
                for gi, b in enumerate(grp):
                    wdma(wdg[:, 2 * gi:2 * gi + 2, :], wdv[:, 2 * b:2 * b + 2, :], Bwd[gi])
                for gi, b in enumerate(grp):
                    if b + PF < NB:
                        load_blk(b + PF)
                    s = b % NSL
                    wg, Bwg = wgs[s]
                    wu, Bwu = wus[s]
                    for jj in range(2):
                        j = 2 * gi + jj
                        for tb in range(NTB):
                            gp, Bgp = gps.next()
                            up, Bup = ups.next()
                            for k in range(KC):
                                P.pe(lambda e, k=k, gp=gp, wg=wg, jj=jj, tb=tb: e.matmul(
                                    gp[:, :], lhsT=wg[:, k, jj * 128:(jj + 1) * 128], rhs=xnT[:, k, tb * 512:(tb + 1) * 512],
                                    start=(k == 0), stop=(k == KC - 1)), reads=[Bwg] + xT_bufs(tb), writes=[Bgp])
                            for k in range(KC):
                                P.pe(lambda e, k=k, up=up, wu=wu, jj=jj, tb=tb: e.matmul(
                                    up[:, :], lhsT=wu[:, k, jj * 128:(jj + 1) * 128], rhs=xnT[:, k, tb * 512:(tb + 1) * 512],
                                    start=(k == 0), stop=(k == KC - 1)), reads=[Bwu] + xT_bufs(tb), writes=[Bup])
                            sg, Bsg = sgs.next()
                            P.act(lambda e, sg=sg, gp=gp: e.activation(out=sg, in_=gp[:, :], func=AF.Silu),
                                  reads=[Bgp], writes=[Bsg])
                            P.dve(lambda e, sg=sg, up=up, j=j, tb=tb: e.tensor_tensor(
                                out=hT[:, j, tb * 512:(tb + 1) * 512], in0=sg, in1=up[:, :], op=ALU.mult),
                                reads=[Bsg, Bup], writes=[BhT[j][tb]])
                nj = 2 * len(grp)
                for tt in range(NT):
                    for dh in range(2):
                        yp, Byp = yps.next()
                        for j in range(nj):
                            P.pe(lambda e, j=j, yp=yp, tt=tt, dh=dh: e.matmul(
                                yp[:, :], lhsT=hT[:, j, tt * 128:(tt + 1) * 128], rhs=wdg[:, j, dh * 512:(dh + 1) * 512],
                                start=(j == 0), stop=(j == nj - 1)),
                                reads=[BhT[j][tt // 4], Bwd[j // 2]], writes=[Byp])
                        P.dve(lambda e, yp=yp, tt=tt, dh=dh: e.scalar_tensor_tensor(
                            out=x_sb[:, tt, dh * 512:(dh + 1) * 512], in0=yp[:, :], scalar=0.5,
                            in1=x_sb[:, tt, dh * 512:(dh + 1) * 512], op0=ALU.mult, op1=ALU.add),
                            reads=[Byp, Bx[tt][dh]], writes=[Bx[tt][dh]])

        def mixer(l):
            P.barrier()
            AR.reset()
            norm_T(Wd["g_mix"][l:l + 1, :], x_tile, x_bufs, NT, xnT, BxT)
            P.barrier()
            AR.reset()
            winv = Wd["w_in"][l].rearrange("(c p) n -> p c n", p=128)
            woutv = Wd["w_out"][l].rearrange("(c p) d -> p c d", p=128)
            cv = AR.alloc([NCV], F32)
            Bcv = Buf("cv")
            sdma(cv, Wd["cvec"][l], Bcv)
            vsl = Rot([(AR.alloc([NT, 128], BF16), Buf("v%d" % i)) for i in range(2)])
            mixs = Rot([(AR.alloc([S], BF16), [Buf("mix%d_%d" % (i, tb)) for tb in range(NTB)]) for i in range(2)])
            wsl = Rot([(AR.alloc([KC, 128], BF16), Buf("wsl%d" % i)) for i in range(6)])
            wos = Rot([(AR.alloc([D], BF16), Buf("wo%d" % i)) for i in range(2)])
            common_end = AR.off
            prj = Rot([(pbank[6], Bpb[6]), (pbank[7], Bpb[7])])

            class BG:
                def __init__(self):
                    self.q = []
                    self.timed = []

                def add(self, th):
                    self.q.append(th)

                def add_timed(self, delay, th):
                    self.timed.append([delay, th])

                def step(self, n=1):
                    fire = [t for t in self.timed if t[0] <= 0]
                    self.timed = [t for t in self.timed if t[0] > 0]
                    for t in self.timed:
                        t[0] -= 1
                    for t in fire:
                        t[1]()
                    for _ in range(n):
                        if self.q:
                            self.q.pop(0)()

                def step_auto(self, remaining):
                    n = -(-len(self.q) // max(1, remaining))
                    self.step(max(1, n) if self.q else 0)

                def flush(self):
                    for t in self.timed:
                        t[1]()
                    self.timed = []
                    while self.q:
                        self.q.pop(0)()

            bg = BG()

            def proj_fm(col0, evac):
                w, Bw = wsl.next()
                first = [True]

                def mk(tb):
                    hold = {}

                    def sub(j):
                        def th():
                            if first[0]:
                                wdma(w, winv[:, :, col0:col0 + 128], Bw)
                                first[0] = False
                            if j == 0:
                                hold["pp"] = prj.next()
                            pp, Bpp = hold["pp"]
                            for k in (2 * j, 2 * j + 1):
                                P.pe(lambda e, k=k: e.matmul(pp[:, :], lhsT=w[:, k, :], rhs=xnT[:, k, tb * 512:(tb + 1) * 512],
                                                             start=(k == 0), stop=(k == KC - 1)), reads=[Bw] + xT_bufs(tb), writes=[Bpp])
                            if j == 3:
                                evac(tb, pp, Bpp)
                        return th
                    return [sub(j) for j in range(4)]
                for tb in range(NTB):
                    for th in mk(tb):
                        bg.add(th)

            def proj_tm(col0, v, Bv):
                w, Bw = wsl.next()
                first = [True]

                def mk(t4):
                    hold = {}

                    def sub(i):
                        def th():
                            if first[0]:
                                wdma(w, winv[:, :, col0:col0 + 128], Bw)
                                first[0] = False
                            if i == 0:
                                hold["pp"] = prj.next()
                            pp, Bpp = hold["pp"]
                            tt = t4 * 4 + i
                            for k in range(KC):
                                P.pe(lambda e, k=k: e.matmul(
                                    pp[:, i * 128:(i + 1) * 128], lhsT=xnT[:, k, tt * 128:(tt + 1) * 128], rhs=w[:, k, :],
                                    start=(k == 0), stop=(k == KC - 1)), reads=[Bw, BxT[tt]], writes=[Bpp])
                            if i == 3:
                                P.act(lambda e: e.copy(out=v[:, t4 * 4:t4 * 4 + 4, :], in_=pp[:, :].rearrange("p (a b) -> p a b", a=4)),
                                      reads=[Bpp], writes=[Bv])
                        return th
                    return [sub(i) for i in range(4)]
                for t4 in range(NT // 4):
                    for th in mk(t4):
                        bg.add(th)

            def out_proj(c, mix, Bmix):
                wo, Bwo = wos.next()
                first = [True]

                def mk(tt, dh):
                    def th():
                        if first[0]:
                            wdma(wo, woutv[:, c, :], Bwo)
                            first[0] = False
                        yp, Byp = prj.next()
                        P.pe(lambda e: e.matmul(yp[:, :], lhsT=mix[:, tt * 128:(tt + 1) * 128], rhs=wo[:, dh * 512:(dh + 1) * 512],
                                                start=True, stop=True), reads=[Bmix[tt // 4], Bwo], writes=[Byp])
                        P.dve(lambda e: e.tensor_tensor(out=x_sb[:, tt, dh * 512:(dh + 1) * 512], in0=yp[:, :],
                                                        in1=x_sb[:, tt, dh * 512:(dh + 1) * 512], op=ALU.add),
                              reads=[Byp, Bx[tt][dh]], writes=[Bx[tt][dh]])
                    return th
                for tt in range(NT):
                    for dh in range(2):
                        bg.add(mk(tt, dh))

            qks = Rot([((AR.alloc([S], BF16), [Buf("q%d_%d" % (i, tb)) for tb in range(NTB)]),
                        (AR.alloc([S], BF16), [Buf("k%d_%d" % (i, tb)) for tb in range(NTB)])) for i in range(2)])
            Es = Rot([(AR.alloc([512], F32), Buf("E%d" % i)) for i in range(3)])
            Sps = Rot([(AR.alloc([512], BF16), Buf("Sp%d" % i)) for i in range(3)])
            Xs = Rot([(AR.alloc([512], F32), Buf("X%d" % i)) for i in range(2)])
            As = Rot([(AR.alloc([512], BF16), Buf("A%d" % i)) for i in range(3)])
            zps = Rot([(pbank[0], Bpb[0]), (pbank[1], Bpb[1]), (pbank[5], Bpb[5])])
            cps = [(pbank[2], Bpb[2]), (pbank[3], Bpb[3])]
            ops_ = (pbank[4], Bpb[4])
            zsrc_bufs = [BxT[0], BxT[1], BxT[2], BxT[3]]

            def evac_to(dst, Bdst):
                def f(tb, pp, Bpp):
                    P.act(lambda e: e.copy(out=dst[:, tb * 512:(tb + 1) * 512], in_=pp[:, :]), reads=[Bpp], writes=[Bdst[tb]])
                return f

            def sb_sched_proj(p):
                (qT, BqT), (kT, BkT) = qks.next()
                v, Bv = vsl.next()
                proj_fm(0 + p * 128, evac_to(qT, BqT))
                proj_fm(512 + p * 128, evac_to(kT, BkT))
                proj_tm(1024 + p * 128, v, Bv)
                return qT, BqT, kT, BkT, v, Bv

            def sb_attention(qT, BqT, kT, BkT, v, Bv, mix, Bmix):
                for tb in range(NTB):
                    op_, Bop = ops_
                    P.pe(lambda e: e.matmul(op_[:, :], lhsT=zeros_bf[:, :], rhs=xnT[:, 0, 0:512], start=True, stop=True),
                         reads=[Bconst] + zsrc_bufs, writes=[Bop])
                    for e_ in range(2):
                        cp, Bcp = cps[e_]
                        P.pe(lambda e, cp=cp: e.matmul(cp[:, :], lhsT=zeros_bf[:, :], rhs=xnT[:, 0, 0:512], start=True, stop=True),
                             reads=[Bconst] + zsrc_bufs, writes=[Bcp])
                    units = []
                    for c in range(4 * tb + 3, -1, -1):
                        for e_ in range(2):
                            units.append((c, e_))
                    nU = len(units)
                    state = [None] * nU

                    def stage0(u):
                        c, e_ = units[u]
                        r = c - 4 * tb
                        off = 128 * r if r >= 0 else 0
                        zp, Bzp = zps.next()
                        E, BE = Es.next()
                        Sp, BSp = Sps.next()
                        hs = slice(64 * e_, 64 * e_ + 64)
                        P.pe(lambda e: e.matmul(zp[:, off:512], lhsT=kT[hs, c * 128:(c + 1) * 128],
                                                rhs=qT[hs, tb * 512 + off:(tb + 1) * 512], start=True, stop=True),
                             reads=[BkT[c // 4], BqT[tb]], writes=[Bzp])
                        P.act(lambda e: e.activation(out=E[:, off:512], in_=zp[:, off:512], func=AF.Exp, scale=0.125),
                              reads=[Bzp], writes=[BE])
                        P.act(lambda e: e.activation(out=Sp[:, off:512], in_=E[:, off:512], func=AF.Ln, bias=1.0),
                              reads=[BE], writes=[BSp])
                        if r >= 0:
                            P.dve(lambda e: e.tensor_tensor(out=Sp[:, off:off + 128], in0=Sp[:, off:off + 128],
                                                            in1=m_strict[:, :], op=ALU.mult),
                                  reads=[BSp, Bconst], writes=[BSp])
                        state[u] = (c, e_, r, off, E, BE, Sp, BSp)

                    def stage1(u):
                        c, e_, r, off, E, BE, Sp, BSp = state[u]
                        cp, Bcp = cps[e_]
                        X, BX = Xs.next()
                        A, BA = As.next()
                        P.pe(lambda e: e.matmul(cp[:, off:512], lhsT=l_incl[:, :], rhs=Sp[:, off:512], start=False, stop=True),
                             reads=[Bconst, BSp], writes=[Bcp])
                        P.act(lambda e: e.activation(out=X[:, off:512], in_=cp[:, off:512], func=AF.Exp, scale=-1.0),
                              reads=[Bcp], writes=[BX])
                        P.dve(lambda e: e.tensor_tensor(out=A[:, off:512], in0=E[:, off:512], in1=X[:, off:512], op=ALU.mult),
                              reads=[BE, BX], writes=[BA])
                        if r >= 0:
                            P.dve(lambda e: e.tensor_tensor(out=A[:, off:off + 128], in0=A[:, off:off + 128],
                                                            in1=m_strict[:, :], op=ALU.mult),
                                  reads=[BA, Bconst], writes=[BA])
                        state[u] = state[u] + (A, BA)

                    def stage2(u):
                        c, e_, r, off, E, BE, Sp, BSp, A, BA = state[u]
                        cp, Bcp = cps[e_]
                        if c > 0:
                            P.pe(lambda e: e.matmul(cp[:, off:512], lhsT=m_strict[:, :], rhs=Sp[:, off:512], start=False, stop=True),
                                 reads=[Bconst, BSp], writes=[Bcp])
                        P.pe(lambda e: e.matmul(op_[64 * e_:64 * e_ + 64, off:512], lhsT=v[:, c, 64 * e_:64 * e_ + 64],
                                                rhs=A[:, off:512], start=False, stop=True),
                             reads=[Bv, BA], writes=[Bop])

                    for step in range(nU + 2):
                        if step < nU:
                            stage0(step)
                        if 0 <= step - 1 < nU:
                            stage1(step - 1)
                        if 0 <= step - 2 < nU:
                            stage2(step - 2)
                        sb_left[0] -= 1
                        bg.step_auto(sb_left[0])
                    P.act(lambda e: e.copy(out=mix[:, tb * 512:(tb + 1) * 512], in_=op_[:, :]),
                          reads=[Bop], writes=[Bmix[tb]])

            sb_left = [88]
            if stop_after != "skip_sb":
                nxt = sb_sched_proj(0)
                bg.flush()
                for p in range(4):
                    cur = nxt
                    if p + 1 < 4:
                        nxt = sb_sched_proj(p + 1)
                    mix, Bmix = mixs.next()
                    sb_left[0] = 88
                    sb_attention(*cur, mix, Bmix)
                    bg.flush()
                    out_proj(p, mix, Bmix)
                bg.flush()

            P.barrier()
            AR.reset(common_end)
            T1 = AR.alloc([S], F32)
            T2 = AR.alloc([S], F32)
            T3 = AR.alloc([S], F32)
            BT1 = [Buf("T1_%d" % tb) for tb in range(NTB)]
            BT2 = [Buf("T2_%d" % tb) for tb in range(NTB)]
            BT3 = [Buf("T3_%d" % tb) for tb in range(NTB)]
            wgp = AR.alloc([KC, 256], BF16)
            Bwgp = Buf("wgp")
            qa = AR.alloc([S], BF16)
            ka = AR.alloc([S], BF16)
            t3b = T3.bitcast(BF16)
            qb = t3b[:, 0:S]
            kb = t3b[:, S:2 * S]
            qkm = Rot([((qa, [Buf("mqa%d" % tb) for tb in range(NTB)]), (ka, [Buf("mka%d" % tb) for tb in range(NTB)]), None),
                       ((qb, [Buf("mqb%d" % tb) for tb in range(NTB)]), (kb, [Buf("mkb%d" % tb) for tb in range(NTB)]), BT3)])
            sga = AR.alloc([S], BF16)
            sgb = wgp.rearrange("p a b -> p (a b)")
            sgs = Rot([(sga, [Buf("sga%d" % tb) for tb in range(NTB)], None), (sgb, [Buf("sgb%d" % tb) for tb in range(NTB)], [Bwgp])])
            pre = AR.alloc([S], BF16)
            Bpre = [Buf("pre%d" % tb) for tb in range(NTB)]
            yc = AR.alloc([S], F32)
            Byc = Buf("yc")
            Ws = Rot([(AR.alloc([512], F32), Buf("W%d" % i)) for i in range(3)])
            Pms = Rot([(AR.alloc([512], BF16), Buf("Pm%d" % i)) for i in range(5)])
            dab = AR.alloc([512], F32)
            Bdab = Buf("dab")
            hTt = AR.alloc([512], F32)
            BhTt = Buf("hTt")
            sq = AR.alloc([512], BF16)
            Bsq = Buf("sq")
            rsd = AR.alloc([512], F32)
            Brsd = Buf("rsd")
            gcol = AR.alloc([64], F32)
            Bgcol = Buf("gcol")
            nbias = AR.alloc([2], F32)
            Bnb = Buf("nbias")

            def conv_silu(cc, dst, Bdst, extra_w):
                def w(j):
                    return cv[:, CV_WCONV + cc * 4 + j:CV_WCONV + cc * 4 + j + 1]

                def th1():
                    P.dve(lambda e: e.tensor_scalar(out=yc, in0=pre, scalar1=w(3), scalar2=None, op0=ALU.mult), reads=Bpre + [Bcv], writes=[Byc])
                    P.dve(lambda e: e.scalar_tensor_tensor(out=yc[:, 1:S], in0=pre[:, 0:S - 1], scalar=w(2), in1=yc[:, 1:S],
                                                           op0=ALU.mult, op1=ALU.add), reads=Bpre + [Bcv, Byc], writes=[Byc])

                def th2():
                    for sh, j in ((2, 1), (3, 0)):
                        P.dve(lambda e, sh=sh, j=j: e.scalar_tensor_tensor(out=yc[:, sh:S], in0=pre[:, 0:S - sh], scalar=w(j), in1=yc[:, sh:S],
                                                                           op0=ALU.mult, op1=ALU.add), reads=Bpre + [Bcv, Byc], writes=[Byc])

                def th3():
                    for tb in range(NTB):
                        sl = slice(tb * 512, (tb + 1) * 512)
                        P.act(lambda e, sl=sl: e.activation(out=dst[:, sl], in_=yc[:, sl], func=AF.Silu, bias=cv[:, CV_BCONV + cc:CV_BCONV + cc + 1]),
                              reads=[Byc, Bcv], writes=[Bdst[tb]] + (extra_w if extra_w else []))
                bg.add(th1)
                bg.add(th2)
                bg.add(th3)

            def evac_pre(tb, pp, Bpp):
                P.act(lambda e: e.copy(out=pre[:, tb * 512:(tb + 1) * 512], in_=pp[:, :]), reads=[Bpp], writes=[Bpre[tb]])

            def ml_sched_proj(h):
                (qT, BqT), (kT, BkT), extra = qkm.next()
                sg, Bsg, extra_s = sgs.next()
                v, Bv = vsl.next()

                def evac_sig(tb, pp, Bpp):
                    P.act(lambda e: e.activation(out=sg[:, tb * 512:(tb + 1) * 512], in_=pp[:, :], func=AF.Sigmoid), reads=[Bpp],
                          writes=[Bsg[tb]] + (extra_s if extra_s else []))
                proj_fm(1536 + h * 128, evac_pre)
                conv_silu(h, qT, BqT, extra)
                proj_fm(2048 + h * 128, evac_pre)
                conv_silu(4 + h, kT, BkT, extra)
                proj_tm(2560 + h * 128, v, Bv)
                proj_fm(3072 + h * 128, evac_sig)
                return qT, BqT, kT, BkT, v, Bv, sg, Bsg

            ml_left = [52]
            nxt = ml_sched_proj(0)

            wdma(wgp, Wd["wgp"][l].rearrange("(c p) n -> p c n", p=128), Bwgp)
            P.dve(lambda e: e.tensor_scalar(out=nbias[:, 0:1], in0=cv[:, CV_BF:CV_BF + 1], scalar1=-1.0, scalar2=None, op0=ALU.mult),
                  reads=[Bcv], writes=[Bnb])
            gi_ps = [(pbank[0], Bpb[0]), (pbank[1], Bpb[1]), (pbank[2], Bpb[2]), (pbank[3], Bpb[3])]
            for tb in range(NTB):
                pp, Bpp = prj.next()
                for k in range(KC):
                    P.pe(lambda e, k=k, pp=pp, tb=tb: e.matmul(pp[:, :], lhsT=wgp[:, k, 128:256], rhs=xnT[:, k, tb * 512:(tb + 1) * 512],
                                                              start=(k == 0), stop=(k == KC - 1)), reads=[Bwgp] + xT_bufs(tb), writes=[Bpp])
                P.act(lambda e, pp=pp, tb=tb: e.activation(out=T1[:, tb * 512:(tb + 1) * 512], in_=pp[:, :], func=AF.Exp, scale=-1.0,
                                                           bias=nbias[:, 0:1]), reads=[Bpp, Bnb], writes=[BT1[tb]])
                P.act(lambda e, tb=tb: e.activation(out=T1[:, tb * 512:(tb + 1) * 512], in_=T1[:, tb * 512:(tb + 1) * 512], func=AF.Ln, bias=1.0),
                      reads=[BT1[tb]], writes=[BT1[tb]])
                gp_, Bgp_ = gi_ps[tb]
                for k in range(KC):
                    P.pe(lambda e, k=k, gp_=gp_, tb=tb: e.matmul(gp_[:, :], lhsT=wgp[:, k, 0:128], rhs=xnT[:, k, tb * 512:(tb + 1) * 512],
                                                                start=(k == 0), stop=(k == KC - 1)), reads=[Bwgp] + xT_bufs(tb), writes=[Bgp_])
            P.dve(lambda e: e.tensor_tensor_scan(out=T2, data0=ones_f[:, 0:1].broadcast_to([128, S]), data1=T1, initial=0.0,
                                                 op0=ALU.mult, op1=ALU.add), reads=BT1 + [Bconst], writes=BT2)
            for tb in range(NTB):
                gp_, Bgp_ = gi_ps[tb]
                P.dve(lambda e, gp_=gp_, tb=tb: e.scalar_tensor_tensor(out=T3[:, tb * 512:(tb + 1) * 512], in0=gp_[:, :],
                                                                       scalar=cv[:, CV_BI:CV_BI + 1], in1=T2[:, tb * 512:(tb + 1) * 512],
                                                                       op0=ALU.add, op1=ALU.add),
                      reads=[Bgp_, Bcv, BT2[tb]], writes=[BT3[tb]])
            bg.step(38)
            gcp, Bgcp = pbank[0], Bpb[0]
            for c in range(NT):
                P.pe(lambda e, c=c: e.matmul(gcp[:, c * 4:(c + 1) * 4], lhsT=T3[:, c * 128:(c + 1) * 128], rhs=selc[:, :], start=True, stop=True),
                     reads=[BT3[c // 4], Bconst], writes=[Bgcp])
            P.act(lambda e: e.copy(out=gcol, in_=gcp[:, 0:64]), reads=[Bgcp], writes=[Bgcol])
            P.dve(lambda e: e.tensor_tensor_scan(out=T1, data0=zeros_f[:, 0:1].broadcast_to([128, S]), data1=T3, initial=0.0,
                                                 op0=ALU.add, op1=ALU.max), reads=BT3 + [Bconst], writes=BT1)
            for tb in range(NTB):
                sl = slice(tb * 512, (tb + 1) * 512)
                P.dve(lambda e, sl=sl: e.tensor_tensor(out=T2[:, sl], in0=T2[:, sl], in1=T1[:, sl], op=ALU.subtract),
                      reads=[BT2[tb], BT1[tb]], writes=[BT2[tb]])
                P.act(lambda e, sl=sl: e.activation(out=T2[:, sl], in_=T2[:, sl], func=AF.Exp), reads=[BT2[tb]], writes=[BT2[tb]])
                P.dve(lambda e, sl=sl: e.tensor_scalar(out=T1[:, sl], in0=T1[:, sl], scalar1=-1.0, scalar2=None, op0=ALU.mult),
                      reads=[BT1[tb]], writes=[BT1[tb]])

            sps = Rot([(pbank[0], Bpb[0]), (pbank[1], Bpb[1]), (pbank[5], Bpb[5])])
            ngp, Bngp = pbank[2], Bpb[2]
            nump, Bnump = pbank[3], Bpb[3]
            denp, Bdenp = pbank[4], Bpb[4]
            QSCALE = 128.0 ** -0.5

            def ml_attention(h, qT, BqT, kT, BkT, v, Bv, sg, Bsg, mix, Bmix):
                for tb in range(NTB):
                    tsl = slice(tb * 512, (tb + 1) * 512)
                    P.pe(lambda e: e.matmul(ngp[:, :], lhsT=selh[h][:, :], rhs=T1[:, tsl], start=True, stop=True),
                         reads=[Bconst, BT1[tb]], writes=[Bngp])
                    nC = 4 * tb + 4
                    state = [None] * nC

                    def stage0(c):
                        r = c - 4 * tb
                        off = 128 * r if r >= 0 else 0
                        sp_, Bsp_ = sps.next()
                        W_, BW = Ws.next()
                        Pm, BPm = Pms.next()
                        P.pe(lambda e: e.matmul(sp_[:, off:512], lhsT=kT[:, c * 128:(c + 1) * 128],
                                                rhs=qT[:, tb * 512 + off:(tb + 1) * 512], start=True, stop=True),
                             reads=[BkT[c // 4], BqT[tb]], writes=[Bsp_])
                        P.act(lambda e: e.activation(out=W_[:, off:512], in_=ngp[:, off:512], func=AF.Exp,
                                                     bias=gcol[:, c * 4 + h:c * 4 + h + 1]),
                              reads=[Bngp, Bgcol], writes=[BW])
                        P.dve(lambda e: e.scalar_tensor_tensor(out=Pm[:, off:512], in0=sp_[:, off:512], scalar=QSCALE,
                                                               in1=W_[:, off:512], op0=ALU.mult, op1=ALU.mult),
                              reads=[Bsp_, BW], writes=[BPm])
                        if r >= 0:
                            P.dve(lambda e: e.tensor_tensor(out=Pm[:, off:off + 128], in0=Pm[:, off:off + 128], in1=m_incl[:, :], op=ALU.mult),
                                  reads=[BPm, Bconst], writes=[BPm])
                        state[c] = (off, Pm, BPm)

                    def stage1(c):
                        off, Pm, BPm = state[c]
                        P.pe(lambda e: e.matmul(nump[:, off:512], lhsT=v[:, c, :], rhs=Pm[:, off:512], start=(c == 0), stop=True),
                             reads=[Bv, BPm], writes=[Bnump])
                        P.pe(lambda e: e.matmul(denp[:, off:512], lhsT=ones_bf[:, :], rhs=Pm[:, off:512], start=(c == 0), stop=True),
                             reads=[Bconst, BPm], writes=[Bdenp])

                    for step in range(nC + 3):
                        if step < nC:
                            stage0(step)
                        if 0 <= step - 3 < nC:
                            stage1(step - 3)
                        ml_left[0] -= 1
                        bg.step_auto(ml_left[0])

                    ep, Bep = sps.next()
                    P.pe(lambda e: e.matmul(ep[:, :], lhsT=selh[h][:, :], rhs=T2[:, tsl], start=True, stop=True),
                         reads=[Bconst, BT2[tb]], writes=[Bep])
                    P.act(lambda e: e.activation(out=dab, in_=denp[:, :], func=AF.Abs), reads=[Bdenp], writes=[Bdab])
                    P.dve(lambda e: e.tensor_tensor(out=dab, in0=dab, in1=ep[:, :], op=ALU.max), reads=[Bdab, Bep], writes=[Bdab])
                    P.act(lambda e: e.activation(out=dab, in_=dab, func=AF.Ln), reads=[Bdab], writes=[Bdab])
                    P.act(lambda e: e.activation(out=dab, in_=dab, func=AF.Exp, scale=-1.0), reads=[Bdab], writes=[Bdab])
                    P.dve(lambda e: e.tensor_tensor(out=hTt, in0=nump[:, :], in1=dab, op=ALU.mult), reads=[Bnump, Bdab], writes=[BhTt])
                    P.act(lambda e: e.activation(out=sq, in_=hTt, func=AF.Square), reads=[BhTt], writes=[Bsq])

                    def postB(tb=tb, tsl=tsl):
                        mp, Bmp = sps.next()
                        P.pe(lambda e: e.matmul(mp[:, :], lhsT=ones_bf[:, :], rhs=sq, start=True, stop=True), reads=[Bconst, Bsq], writes=[Bmp])
                        rstd_from_ms(rsd, mp[:, :], [Bmp], [Brsd], scale=1.0 / 128.0)
                        P.dve(lambda e: e.tensor_tensor(out=hTt, in0=hTt, in1=rsd, op=ALU.mult), reads=[BhTt, Brsd], writes=[BhTt])
                        P.dve(lambda e: e.scalar_tensor_tensor(out=mix[:, tsl], in0=hTt, scalar=cv[:, CV_GHEAD + h:CV_GHEAD + h + 1],
                                                               in1=sg[:, tsl], op0=ALU.mult, op1=ALU.mult),
                              reads=[BhTt, Bcv, Bsg[tb]], writes=[Bmix[tb]])
                    bg.add_timed(3, postB)

            bg.flush()
            for h in range(4):
                cur = nxt
                if h + 1 < 4:
                    nxt = ml_sched_proj(h + 1)
                mix, Bmix = mixs.next()
                ml_left[0] = 52
                ml_attention(h, *cur, mix, Bmix)
                bg.flush()
                out_proj(4 + h, mix, Bmix)
            bg.flush()


        def xattn(l):
            P.barrier()
            AR.reset()
            norm_T(Wd["g_xattn"][l:l + 1, :], x_tile, x_bufs, NT, xnT, BxT)
            P.barrier()
            AR.reset()
            cv = AR.alloc([NCV], F32)
            Bcv = Buf("cvx")
            sdma(cv, Wd["cvec"][l], Bcv)
            knT = AR.alloc([KC, MEM], BF16)
            BknT = [Buf("knT%d" % c) for c in range(KC)]
            vx = AR.alloc([2, D], BF16)
            Bvx = Buf("vx")
            keep = AR.off
            memx = AR.alloc([2, D], F32)
            Bmem = [Buf("mem%d" % i) for i in range(2)]
            memnT = AR.alloc([KC, MEM], BF16)
            BmT = [Buf("memnT%d" % i) for i in range(2)]
            wk = AR.alloc([KC, D], BF16)
            Bwk = Buf("wk")
            wv = AR.alloc([KC, D], BF16)
            Bwv = Buf("wv")
            kf = [AR.alloc([MEM], F32) for _ in range(2)]
            Bkf = [Buf("kf%d" % i) for i in range(2)]
            ksq = [AR.alloc([MEM], BF16) for _ in range(2)]
            Bksq = [Buf("ksq%d" % i) for i in range(2)]
            krs = AR.alloc([MEM], F32)
            Bkrs = Buf("krs")
            sdma(memx, mem_d.rearrange("(t p) d -> p t d", p=128), Bmem[0])
            wdma(wk, Wd["w_xk"][l].rearrange("(c p) n -> p c n", p=128), Bwk)
            wdma(wv, Wd["w_xv"][l].rearrange("(c p) n -> p c n", p=128), Bwv)
            norm_T(Wd["g_mem"][l:l + 1, :], lambda tt: memx[:, tt, :], lambda tt: [Bmem[0]], 2, memnT, BmT)
            sps = Rot([(pbank[0], Bpb[0]), (pbank[1], Bpb[1])])
            ms_ps, Bms = pbank[2], Bpb[2]
            for h in range(4):
                for fc in range(2):
                    f = 2 * h + fc
                    kp, Bkp = sps.next()
                    for k in range(KC):
                        P.pe(lambda e, k=k, f=f, kp=kp: e.matmul(kp[:, 0:MEM], lhsT=wk[:, k, f * 128:(f + 1) * 128], rhs=memnT[:, k, :],
                                                                start=(k == 0), stop=(k == KC - 1)), reads=[Bwk] + BmT, writes=[Bkp])
                    P.act(lambda e, fc=fc, kp=kp: e.copy(out=kf[fc], in_=kp[:, 0:MEM]), reads=[Bkp], writes=[Bkf[fc]])
                    P.act(lambda e, fc=fc, kp=kp: e.activation(out=ksq[fc], in_=kp[:, 0:MEM], func=AF.Square), reads=[Bkp], writes=[Bksq[fc]])
                for fc in range(2):
                    P.pe(lambda e, fc=fc: e.matmul(ms_ps[:, 0:MEM], lhsT=ones_bf[:, :], rhs=ksq[fc], start=(fc == 0), stop=(fc == 1)),
                         reads=[Bconst, Bksq[fc]], writes=[Bms])
                rstd_from_ms(krs, ms_ps[:, 0:MEM], [Bms], [Bkrs], scale=1.0 / 256.0)
                for fc in range(2):
                    f = 2 * h + fc
                    P.dve(lambda e, fc=fc, f=f: e.scalar_tensor_tensor(out=knT[:, f, :], in0=kf[fc], scalar=cv[:, CV_GK + fc:CV_GK + fc + 1],
                                                                      in1=krs, op0=ALU.mult, op1=ALU.mult),
                          reads=[Bkf[fc], Bcv, Bkrs], writes=[BknT[f]])
            for mt in range(2):
                for dh in range(2):
                    vp, Bvp = sps.next()
                    for k in range(KC):
                        P.pe(lambda e, k=k, mt=mt, dh=dh, vp=vp: e.matmul(vp[:, :], lhsT=memnT[:, k, mt * 128:(mt + 1) * 128],
                                                                         rhs=wv[:, k, dh * 512:(dh + 1) * 512], start=(k == 0), stop=(k == KC - 1)),
                             reads=[Bwv] + BmT, writes=[Bvp])
                    P.act(lambda e, mt=mt, dh=dh, vp=vp: e.copy(out=vx[:, mt, dh * 512:(dh + 1) * 512], in_=vp[:, :]), reads=[Bvp], writes=[Bvx])
            P.barrier()
            AR.reset(keep)
            wq = AR.alloc([KC, D], BF16)
            Bwq = Buf("wq")
            wo = AR.alloc([KC, D], BF16)
            Bwo = Buf("wo")
            wdma(wq, Wd["w_xq"][l].rearrange("(c p) n -> p c n", p=128), Bwq)
            wdma(wo, Wd["w_xo"][l].rearrange("(c p) n -> p c n", p=128), Bwo)

            def qset(i):
                return dict(qf=[AR.alloc([512], F32) for _ in range(2)], Bqf=[Buf("qf%d_%d" % (i, j)) for j in range(2)],
                            qsq=[AR.alloc([512], BF16) for _ in range(2)], Bqsq=[Buf("qsq%d_%d" % (i, j)) for j in range(2)],
                            qn=[AR.alloc([512], BF16) for _ in range(2)], Bqn=[Buf("qn%d_%d" % (i, j)) for j in range(2)],
                            qrs=AR.alloc([512], F32), Bqrs=Buf("qrs%d" % i))
            qsets = Rot([qset(i) for i in range(2)])
            Exs = Rot([([AR.alloc([512], BF16) for _ in range(2)], [Buf("Ex%d_%d" % (i, j)) for j in range(2)]) for i in range(2)])
            recs = Rot([(AR.alloc([512], F32), Buf("rec%d" % i)) for i in range(2)])
            aTs = Rot([(AR.alloc([KC, 512], BF16), [Buf("aT%d_%d" % (i, c)) for c in range(KC)]) for i in range(2)])
            qps = Rot([(pbank[0], Bpb[0]), (pbank[1], Bpb[1])])
            scps = Rot([(pbank[2], Bpb[2]), (pbank[3], Bpb[3])])
            dnp, Bdnp = pbank[4], Bpb[4]
            ops2 = Rot([(pbank[6], Bpb[6]), (pbank[7], Bpb[7])])
            yps = Rot([(pbank[5], Bpb[5])])

            def stageA(tb, h):
                tsl = slice(tb * 512, (tb + 1) * 512)
                qs = qsets.next()
                for fc in range(2):
                    f = 2 * h + fc
                    qp, Bqp = qps.next()
                    for k in range(KC):
                        P.pe(lambda e, k=k: e.matmul(qp[:, :], lhsT=wq[:, k, f * 128:(f + 1) * 128], rhs=xnT[:, k, tsl],
                                                     start=(k == 0), stop=(k == KC - 1)), reads=[Bwq] + xT_bufs(tb), writes=[Bqp])
                    P.act(lambda e: e.copy(out=qs["qf"][fc], in_=qp[:, :]), reads=[Bqp], writes=[qs["Bqf"][fc]])
                    P.act(lambda e: e.activation(out=qs["qsq"][fc], in_=qp[:, :], func=AF.Square), reads=[Bqp], writes=[qs["Bqsq"][fc]])
                mp, Bmp = qps.next()
                for fc in range(2):
                    P.pe(lambda e, fc=fc: e.matmul(mp[:, :], lhsT=ones_bf[:, :], rhs=qs["qsq"][fc], start=(fc == 0), stop=(fc == 1)),
                         reads=[Bconst, qs["Bqsq"][fc]], writes=[Bmp])
                rstd_from_ms(qs["qrs"], mp[:, :], [Bmp], [qs["Bqrs"]], scale=1.0 / 256.0)
                for fc in range(2):
                    P.dve(lambda e, fc=fc: e.scalar_tensor_tensor(out=qs["qn"][fc], in0=qs["qf"][fc], scalar=cv[:, CV_GQ + fc:CV_GQ + fc + 1],
                                                                 in1=qs["qrs"], op0=ALU.mult, op1=ALU.mult),
                          reads=[qs["Bqf"][fc], Bcv, qs["Bqrs"]], writes=[qs["Bqn"][fc]])
                return qs

            def stageB(tb, h, qs, attnT, BaT):
                Ex, BEx = Exs.next()
                rec, Brec = recs.next()
                for mc in range(2):
                    sp_, Bsp_ = scps.next()
                    for fc in range(2):
                        P.pe(lambda e, fc=fc: e.matmul(sp_[:, :], lhsT=knT[:, 2 * h + fc, mc * 128:(mc + 1) * 128], rhs=qs["qn"][fc],
                                                       start=(fc == 0), stop=(fc == 1)), reads=[BknT[2 * h + fc], qs["Bqn"][fc]], writes=[Bsp_])
                    P.act(lambda e: e.activation(out=Ex[mc], in_=sp_[:, :], func=AF.Exp, scale=1.0 / 16.0), reads=[Bsp_], writes=[BEx[mc]])
                for mc in range(2):
                    P.pe(lambda e, mc=mc: e.matmul(dnp[:, :], lhsT=ones_bf[:, :], rhs=Ex[mc], start=(mc == 0), stop=(mc == 1)),
                         reads=[Bconst, BEx[mc]], writes=[Bdnp])
                P.act(lambda e: e.activation(out=rec, in_=dnp[:, :], func=AF.Ln), reads=[Bdnp], writes=[Brec])
                P.act(lambda e: e.activation(out=rec, in_=rec, func=AF.Exp, scale=-1.0), reads=[Brec], writes=[Brec])
                for dc in range(2):
                    op2, Bop2 = ops2.next()
                    for mc in range(2):
                        P.pe(lambda e, mc=mc: e.matmul(op2[:, :], lhsT=vx[:, mc, h * 256 + dc * 128:h * 256 + (dc + 1) * 128], rhs=Ex[mc],
                                                       start=(mc == 0), stop=(mc == 1)), reads=[Bvx, BEx[mc]], writes=[Bop2])
                    P.dve(lambda e: e.tensor_tensor(out=attnT[:, 2 * h + dc, :], in0=op2[:, :], in1=rec, op=ALU.mult),
                          reads=[Bop2, Brec], writes=[BaT[2 * h + dc]])

            def stageO(tb, attnT, BaT):
                for i in range(4):
                    tt = 4 * tb + i
                    for dh in range(2):
                        yp, Byp = yps.next()
                        for k in range(KC):
                            P.pe(lambda e, k=k: e.matmul(yp[:, :], lhsT=attnT[:, k, i * 128:(i + 1) * 128], rhs=wo[:, k, dh * 512:(dh + 1) * 512],
                                                         start=(k == 0), stop=(k == KC - 1)), reads=[BaT[k], Bwo], writes=[Byp])
                        P.dve(lambda e: e.tensor_tensor(out=x_sb[:, tt, dh * 512:(dh + 1) * 512], in0=yp[:, :],
                                                        in1=x_sb[:, tt, dh * 512:(dh + 1) * 512], op=ALU.add),
                              reads=[Byp, Bx[tt][dh]], writes=[Bx[tt][dh]])

            pending = None
            pendO = None
            for tb in range(NTB):
                aT = aTs.next()
                for h in range(4):
                    qs = stageA(tb, h)
                    if pending is not None:
                        stageB(*pending)
                    if pendO is not None and h == 1:
                        stageO(*pendO)
                        pendO = None
                    pending = (tb, h, qs, aT[0], aT[1])
                pendO_next = (tb, aT[0], aT[1])
                if pendO is not None:
                    stageO(*pendO)
                pendO = pendO_next
            stageB(*pending)
            stageO(*pendO)


        done = False
        for l in range(L):
            ffn(l, "g_ff1", "w_ff1_gate", "w_ff1_up", "w_ff1_down")
            if stop_after == "ffn1":
                break
            mixer(l)
            if stop_after in ("mix", "skip_sb"):
                break
            xattn(l)
            if stop_after == "xattn":
                break
            ffn(l, "g_ff2", "w_ff2_gate", "w_ff2_up", "w_ff2_down")
        P.barrier()
        for c in range(4):
            P.dma("sp", lambda e, c=c: e.dma_start(out=ov[:, 4 * c:4 * c + 4, :], in_=x_sb[:, 4 * c:4 * c + 4, :]),
                  reads=[Bx[t][h] for t in range(4 * c, 4 * c + 4) for h in range(2)], sem_buf=Bout)
        P.finalize(final_wait_bufs=[Bout])
    return nc


def _layer_params(inp, l0, l1):
    f = lambda a: np.ascontiguousarray(np.asarray(a, dtype=np.float32))
    Ln = l1 - l0
    d = {}
    for name, _ in PARAMS:
        if name in ("wgp", "cvec"):
            continue
        d[name] = f(np.asarray(inp[name])[l0:l1])
    w_in = np.asarray(inp["w_in"])[l0:l1]
    wgp = np.zeros((Ln, D, 256), np.float32)
    for h in range(4):
        wgp[:, :, 32 * h] = w_in[:, :, 3584 + h]
        wgp[:, :, 128 + 32 * h] = w_in[:, :, 3588 + h]
    d["wgp"] = wgp
    cv = np.zeros((Ln, 128, NCV), np.float32)
    b_conv = np.asarray(inp["b_conv"])[l0:l1]
    w_conv = np.asarray(inp["w_conv"])[l0:l1]
    g_head = np.asarray(inp["g_mlstm_head"])[l0:l1]
    g_q = np.asarray(inp["g_qnorm"])[l0:l1]
    g_k = np.asarray(inp["g_knorm"])[l0:l1]
    b_gate = np.asarray(inp["b_gate"])[l0:l1]
    cv[:, :, CV_BCONV:CV_BCONV + 8] = b_conv.reshape(Ln, 8, 128).transpose(0, 2, 1)
    cv[:, :, CV_WCONV:CV_WCONV + 32] = w_conv.reshape(Ln, 4, 8, 128).transpose(0, 3, 2, 1).reshape(Ln, 128, 32)
    cv[:, :, CV_GHEAD:CV_GHEAD + 4] = g_head.transpose(0, 2, 1)
    cv[:, :, CV_GQ:CV_GQ + 2] = g_q.reshape(Ln, 2, 128).transpose(0, 2, 1)
    cv[:, :, CV_GK:CV_GK + 2] = g_k.reshape(Ln, 2, 128).transpose(0, 2, 1)
    for h in range(4):
        cv[:, 32 * h, CV_BI] = b_gate[:, h]
        cv[:, 32 * h, CV_BF] = b_gate[:, 4 + h]
    d["cvec"] = cv
    return d


_PROG_CACHE = {}
FUSED = True


def _get_prog(L, stop_after=None):
    key = (L, stop_after)
    if key not in _PROG_CACHE:
        _PROG_CACHE[key] = build_program(L, stop_after)
    return _PROG_CACHE[key]


def run_layers(x, mem, inp, l0, l1, stop_after=None):
    params = _layer_params(inp, l0, l1)
    nc = _get_prog(l1 - l0, stop_after)
    in_maps = []
    for b in range(NCORES):
        m = {"x": np.ascontiguousarray(x[b]), "mem": np.ascontiguousarray(mem[b])}
        m.update(params)
        in_maps.append(m)
    res = run_bass_kernel_spmd(nc, in_maps, core_ids=list(range(NCORES)))
    return np.stack([np.asarray(r["out"]) for r in res.results], axis=0)


def kernel(**inputs):
    x = np.asarray(inputs["x"], dtype=np.float32)
    mem = np.asarray(inputs["mem"], dtype=np.float32)
    if FUSED:
        return run_layers(x, mem, inputs, 0, DEPTH).astype(np.float32)
    for l in range(DEPTH):
        x = run_layers(x, mem, inputs, l, l + 1)
    return x.astype(np.float32)
```
